# Optimizing a Trainium2 kernel written in Bass

```python
import jax
import jax.numpy as jnp
from jax import lax
import numpy as np


D_MODEL = 1024
BATCH = 4
SEQ = 8192
DEPTH = 4
DEC_BATCH = 8
DEC_SEQ = 4096
PAST_LEN = 128

A_HEADS = 8
A_DK = 128
A_DV = 128
A_CONV = 5
A_CHUNK = 64
B_GROUPS = ((128, 1), (512, 4), (2048, 16))
B_N_GROUPS = 3
B_HEADS_PER_GROUP = 8
B_HEAD_DIM = 64
B_HEADS = B_N_GROUPS * B_HEADS_PER_GROUP
ROPE_THETA = 10000.0
EPS = 1e-6
NEG_INF = -1e30

A_QKV_COLS = A_HEADS * (2 * A_DK + A_DV)
A_Z_COLS = A_HEADS * A_DV
A_GATE_COLS = 4 * A_HEADS
B_QKV_COLS = 3 * B_HEADS * B_HEAD_DIM
B_OUT_WIDTH = B_HEADS_PER_GROUP * B_HEAD_DIM
B_Z_COLS = B_OUT_WIDTH
MERGE_COLS = 2 * D_MODEL
IN_COLS = A_QKV_COLS + A_Z_COLS + A_GATE_COLS + B_QKV_COLS + B_Z_COLS + MERGE_COLS
SPLIT_POINTS = (A_QKV_COLS,
                A_QKV_COLS + A_Z_COLS,
                A_QKV_COLS + A_Z_COLS + A_GATE_COLS,
                A_QKV_COLS + A_Z_COLS + A_GATE_COLS + B_QKV_COLS,
                A_QKV_COLS + A_Z_COLS + A_GATE_COLS + B_QKV_COLS + B_Z_COLS)

kernel_name = 'hybrid_gdn_dilated_window_encoder'


def rms_norm(x, w):
    xf = x.astype(jnp.float32)
    y = xf * lax.rsqrt(jnp.mean(xf * xf, axis=-1, keepdims=True) + EPS)
    return (y * w.astype(jnp.float32)).astype(x.dtype)


def l2norm(x):
    return x * lax.rsqrt(jnp.sum(x * x, axis=-1, keepdims=True) + EPS)


def rope(x, positions):
    xf = x.astype(jnp.float32)
    hd = x.shape[-1]
    half = hd // 2
    inv = ROPE_THETA ** (-jnp.arange(half, dtype=jnp.float32) / half)
    ang = positions.astype(jnp.float32)[:, None] * inv[None, :]
    bshape = (1, x.shape[1]) + (1,) * (x.ndim - 3) + (half,)
    cos = jnp.cos(ang).reshape(bshape)
    sin = jnp.sin(ang).reshape(bshape)
    x1, x2 = xf[..., :half], xf[..., half:]
    return jnp.concatenate([x1 * cos - x2 * sin, x2 * cos + x1 * sin], axis=-1)


def gated_delta_rule_chunked(q, k, v, g, beta):
    Bn, S, H, dk = q.shape
    dv = v.shape[-1]
    C = A_CHUNK
    N = S // C

    def chunk4(t):
        return t.reshape(Bn, N, C, H, t.shape[-1]).transpose(1, 0, 3, 2, 4)

    def chunk3(t):
        return t.reshape(Bn, N, C, H).transpose(1, 0, 3, 2)

    qc, kc, vc = chunk4(q), chunk4(k), chunk4(v)
    gc = jnp.cumsum(chunk3(g), axis=-1)
    bc = chunk3(beta)
    lower_incl = jnp.tril(jnp.ones((C, C), dtype=bool))
    lower_strict = jnp.tril(jnp.ones((C, C), dtype=bool), -1)
    decay = jnp.exp(jnp.where(lower_incl, gc[..., :, None] - gc[..., None, :], -jnp.inf))
    kb = kc * bc[..., None]
    a_mat = jnp.where(lower_strict, jnp.einsum('nbhid,nbhjd->nbhij', kb, kc) * decay, 0.0)
    t_mat = a_mat + jnp.eye(C, dtype=a_mat.dtype)
    rhs = jnp.concatenate([vc * bc[..., None], kb * jnp.exp(gc)[..., None]], axis=-1)
    sol = lax.linalg.triangular_solve(t_mat, rhs, left_side=True, lower=True, unit_diagonal=True)
    u, w = sol[..., :dv], sol[..., dv:]
    qk = jnp.einsum('nbhid,nbhjd->nbhij', qc, kc) * decay

    def step(state, xs):
        q_i, k_i, u_i, w_i, g_i, qk_i = xs
        v_new = u_i - jnp.einsum('bhcd,bhde->bhce', w_i, state)
        o_i = (jnp.einsum('bhcd,bhde->bhce', q_i * jnp.exp(g_i)[..., None], state)
               + jnp.einsum('bhij,bhje->bhie', qk_i, v_new))
        g_last = g_i[..., -1:]
        state = (state * jnp.exp(g_last)[..., None]
                 + jnp.einsum('bhcd,bhce->bhde', k_i * jnp.exp(g_last - g_i)[..., None], v_new))
        return state, o_i

    state0 = jnp.zeros((Bn, H, dk, dv), jnp.float32)
    _, o = lax.scan(step, state0, (qc, kc, u, w, gc, qk))
    return o.transpose(1, 0, 3, 2, 4).reshape(Bn, S, H, dv)


def gdn_branch(qkv_a, z_a, gates_a, conv_w, a_log, dt_bias, a_norm_w):
    Bn, S, _ = qkv_a.shape
    pad = A_CONV // 2
    qkv = lax.conv_general_dilated(qkv_a, conv_w[:, None, :], window_strides=(1,),
                                   padding=((pad, pad),),
                                   dimension_numbers=('NWC', 'WIO', 'NWC'),
                                   feature_group_count=A_QKV_COLS)
    qkv = jax.nn.silu(qkv).astype(jnp.float32)
    q, k, v = jnp.split(qkv, [A_HEADS * A_DK, 2 * A_HEADS * A_DK], axis=-1)
    q = l2norm(q.reshape(Bn, S, A_HEADS, A_DK)) * (A_DK ** -0.5)
    k = l2norm(k.reshape(Bn, S, A_HEADS, A_DK))
    v = v.reshape(Bn, S, A_HEADS, A_DV)
    gl = gates_a.astype(jnp.float32).reshape(Bn, S, 4, A_HEADS)
    g = -jnp.exp(a_log.astype(jnp.float32)) * jax.nn.softplus(gl[:, :, 0:2] + dt_bias.astype(jnp.float32))
    beta = jax.nn.sigmoid(gl[:, :, 2:4])
    o_fwd = gated_delta_rule_chunked(q, k, v, g[:, :, 0], beta[:, :, 0])
    rev = lambda t: jnp.flip(t, axis=1)
    o_bwd = rev(gated_delta_rule_chunked(rev(q), rev(k), rev(v), rev(g[:, :, 1]), rev(beta[:, :, 1])))
    o = rms_norm(o_fwd + o_bwd, a_norm_w) * jax.nn.silu(z_a.astype(jnp.float32).reshape(Bn, S, A_HEADS, A_DV))
    return o.reshape(Bn, S, A_Z_COLS).astype(qkv_a.dtype)


def dilated_window_attention(q, k, v, dilation, radius):
    Bn, S, H, hd = q.shape
    L = S // dilation
    Qb = radius
    nb = -(-L // Qb)
    Lp = nb * Qb

    def streams(t):
        return t.reshape(Bn, L, dilation, H, hd).transpose(0, 2, 1, 3, 4)

    qs = jnp.pad(streams(q), ((0, 0), (0, 0), (0, Lp - L), (0, 0), (0, 0))).reshape(Bn, dilation, nb, Qb, H, hd)

    def windows(t):
        tp = jnp.pad(streams(t), ((0, 0), (0, 0), (Qb, Lp - L + Qb), (0, 0), (0, 0)))
        tp = tp.reshape(Bn, dilation, nb + 2, Qb, H, hd)
        return jnp.concatenate([tp[:, :, 0:nb], tp[:, :, 1:nb + 1], tp[:, :, 2:nb + 2]], axis=3)

    kw, vw = windows(k), windows(v)
    s = jnp.einsum('bznqhd,bznkhd->bznhqk', qs, kw) * (hd ** -0.5)
    qi = jnp.arange(Qb)[:, None]
    kj = jnp.arange(3 * Qb)[None, :]
    band = jnp.abs(qi - kj + Qb) <= radius
    key_pos = jnp.arange(nb)[:, None] * Qb - Qb + jnp.arange(3 * Qb)[None, :]
    valid = (key_pos >= 0) & (key_pos < L)
    mask = band[None, :, :] & valid[:, None, :]
    s = jnp.where(mask[:, None], s, NEG_INF)
    m = jnp.max(s, axis=-1, keepdims=True)
    p = jnp.exp(s - m)
    den = jnp.sum(p, axis=-1, keepdims=True)
    o = jnp.einsum('bznhqk,bznkhd->bznqhd', p / den, vw)
    lse = (m + jnp.log(den))[..., 0].transpose(0, 1, 2, 4, 3)
    o = o.reshape(Bn, dilation, Lp, H, hd)[:, :, :L].transpose(0, 2, 1, 3, 4).reshape(Bn, S, H, hd)
    lse = lse.reshape(Bn, dilation, Lp, H)[:, :, :L].transpose(0, 2, 1, 3).reshape(Bn, S, H)
    return o, lse


def dilated_branch(qkv_b, z_b, q_norm_w, k_norm_w):
    Bn, S, _ = qkv_b.shape
    qkv = qkv_b.reshape(Bn, S, 3, B_N_GROUPS, B_HEADS_PER_GROUP, B_HEAD_DIM)
    pos = jnp.arange(S)
    q = rope(rms_norm(qkv[:, :, 0], q_norm_w), pos)
    k = rope(rms_norm(qkv[:, :, 1], k_norm_w), pos)
    v = qkv[:, :, 2].astype(jnp.float32)
    outs = []
    lses = []
    for gi, (window, dilation) in enumerate(B_GROUPS):
        o_g, lse_g = dilated_window_attention(q[:, :, gi], k[:, :, gi], v[:, :, gi],
                                              dilation, window // (2 * dilation))
        outs.append(o_g)
        lses.append(lse_g)
    wts = jax.nn.softmax(jnp.stack(lses, axis=0), axis=0)
    o = jnp.einsum('gbsh,gbshd->bshd', wts, jnp.stack(outs, axis=0))
    o = o.reshape(Bn, S, B_OUT_WIDTH) * jax.nn.silu(z_b.astype(jnp.float32))
    return o.astype(qkv_b.dtype)


def layer(x, norm_w, w_in, conv_a, a_log, dt_bias, a_norm_w, q_norm_w, k_norm_w, w_a_out, w_b_out, w_out):
    h = rms_norm(x, norm_w)
    proj = jnp.einsum('bsd,dc->bsc', h, w_in)
    qkv_a, z_a, gates_a, qkv_b, z_b, merge = jnp.split(proj, list(SPLIT_POINTS), axis=-1)
    y_a = jnp.einsum('bsc,cd->bsd', gdn_branch(qkv_a, z_a, gates_a, conv_a, a_log, dt_bias, a_norm_w), w_a_out)
    y_b = jnp.einsum('bsc,cd->bsd', dilated_branch(qkv_b, z_b, q_norm_w, k_norm_w), w_b_out)
    gate_a, gate_b = jnp.split(jax.nn.sigmoid(merge), 2, axis=-1)
    return x + jnp.einsum('bsd,de->bse', gate_a * y_a + gate_b * y_b, w_out)


def setup_inputs(seed: int = 0) -> dict:
    key = jax.random.key(seed)
    ks = jax.random.split(key, 14)
    f32 = jnp.float32
    x_prompt = jax.random.normal(ks[0], (BATCH, SEQ, D_MODEL), f32)
    x_sample = jax.random.normal(ks[1], (DEC_BATCH, DEC_SEQ, D_MODEL), f32)
    norm_w = 1.0 + 0.02 * jax.random.normal(ks[2], (DEPTH, D_MODEL), f32)
    w_in = jax.random.normal(ks[3], (DEPTH, D_MODEL, IN_COLS), f32) * (D_MODEL ** -0.5)
    conv_a = jax.random.normal(ks[4], (DEPTH, A_CONV, A_QKV_COLS), f32) * (A_CONV ** -0.5)
    a_log = jnp.log(jax.random.uniform(ks[5], (DEPTH, 2, A_HEADS), f32, minval=1.0, maxval=16.0))
    dt = jnp.exp(jax.random.uniform(ks[6], (DEPTH, 2, A_HEADS), f32,
                                    minval=float(np.log(1e-3)), maxval=float(np.log(1e-1))))
    dt_bias = dt + jnp.log(-jnp.expm1(-dt))
    a_norm_w = 1.0 + 0.02 * jax.random.normal(ks[7], (DEPTH, A_DV), f32)
    q_norm_w = 1.0 + 0.02 * jax.random.normal(ks[8], (DEPTH, B_HEAD_DIM), f32)
    k_norm_w = 1.0 + 0.02 * jax.random.normal(ks[9], (DEPTH, B_HEAD_DIM), f32)
    w_a_out = jax.random.normal(ks[10], (DEPTH, A_Z_COLS, D_MODEL), f32) * (A_Z_COLS ** -0.5)
    w_b_out = jax.random.normal(ks[11], (DEPTH, B_OUT_WIDTH, D_MODEL), f32) * (B_OUT_WIDTH ** -0.5)
    w_out = jax.random.normal(ks[12], (DEPTH, D_MODEL, D_MODEL), f32) * (D_MODEL ** -0.5)
    return {'x_prompt': x_prompt, 'x_sample': x_sample, 'norm_w': norm_w, 'w_in': w_in,
            'conv_a': conv_a, 'a_log': a_log, 'dt_bias': dt_bias, 'a_norm_w': a_norm_w,
            'q_norm_w': q_norm_w, 'k_norm_w': k_norm_w, 'w_a_out': w_a_out,
            'w_b_out': w_b_out, 'w_out': w_out}


def reference(x_prompt, x_sample, norm_w, w_in, conv_a, a_log, dt_bias, a_norm_w, q_norm_w, k_norm_w,
              w_a_out, w_b_out, w_out):
    def trunk(x):
        for i in range(DEPTH):
            x = layer(x, norm_w[i], w_in[i], conv_a[i], a_log[i], dt_bias[i], a_norm_w[i],
                      q_norm_w[i], k_norm_w[i], w_a_out[i], w_b_out[i], w_out[i])
        return x
    y_prompt = trunk(x_prompt)
    y_sample = trunk(x_sample)
    return (y_prompt, y_sample)
```

```python
import numpy as np
import ml_dtypes
from contextlib import ExitStack
import concourse.bass as bass
import concourse.mybir as mybir
from concourse.bass_utils import run_bass_kernel_spmd

F32 = mybir.dt.float32
BF16 = mybir.dt.bfloat16
AF = mybir.ActivationFunctionType
ALU = mybir.AluOpType
AX = mybir.AxisListType

D = 1024
IN_COLS = 11296
EPS = 1e-6
NEG = -30000.0
SEC_QKVA, SEC_ZA, SEC_GATE, SEC_QKB, SEC_VB, SEC_ZB, SEC_MG = range(7)
CHUNKS = []
for i in range(24): CHUNKS.append((i * 128, 128, SEC_QKVA, i))
for i in range(8): CHUNKS.append((3072 + i * 128, 128, SEC_ZA, i))
CHUNKS.append((4096, 32, SEC_GATE, 0))
for i in range(24): CHUNKS.append((4128 + i * 128, 128, SEC_QKB, i))
for i in range(12): CHUNKS.append((7200 + i * 128, 128, SEC_VB, i))
for i in range(4): CHUNKS.append((8736 + i * 128, 128, SEC_ZB, i))
for i in range(16): CHUNKS.append((9248 + i * 128, 128, SEC_MG, i))
NCH = len(CHUNKS)
WGROUPS = [list(range(0, 8)), list(range(8, 16)), list(range(16, 24)), list(range(24, 32)), [32],
           list(range(33, 41)), list(range(41, 49)), list(range(49, 57)), list(range(57, 61)), list(range(61, 65)),
           list(range(65, 69)), list(range(69, 73)), list(range(73, 81)), list(range(81, 89))]


_PSUM_KEYS = {"ptr", "pg", "PN", "PD", "PS", "PT", "PHa", "PHb"}
for _i in range(8):
    _PSUM_KEYS.update({f"pm{_i}", f"PP{_i}", f"PR{_i}", f"PT{_i}", f"PSC{_i}", f"pms{_i}", f"pya{_i}", f"pyb{_i}", f"pd{_i}"})


def is_psum_key(k):
    if len(k) > 2 and k[0] == "s" and k[1] in "01":
        k = k[2:]
    return k in _PSUM_KEYS


class TK:
    def __init__(self, nc, stack):
        self.nc = nc
        self.stack = stack
        self.E = {}
        for n, e in (("pe", nc.tensor), ("dve", nc.vector), ("act", nc.scalar), ("pool", nc.gpsimd), ("sp", nc.sync)):
            sem = stack.enter_context(nc.semaphore("s_" + n))
            self.E[n] = dict(e=e, sem=sem, cnt=0, known={}, pend=[])
        self.ds = {}
        self.bufs = {}
        self.ninstr = 0

    def dsem(self, name):
        if name not in self.ds:
            self.ds[name] = dict(sem=self.stack.enter_context(self.nc.semaphore("d_" + name)), cnt=0)
        return self.ds[name]

    def _wait(self, en, reads, writes):
        E = self.E[en]
        own = "s_" + en
        need = {}

        def add(tok, raw):
            sname, sem, val = tok
            if sname == own and en == "pe":
                return
            if sname.startswith("d_"):
                val = max(val, self.ds[sname[2:]]["cnt"])
            if need.get(sname, (None, 0))[1] < val:
                need[sname] = (sem, val)

        for k in reads:
            b = self.bufs.get(k)
            if b and b["w"]:
                add(b["w"], True)
            if b and is_psum_key(k):
                for sname, (sem, val) in b["r"].items():
                    if sname != own:
                        add((sname, sem, val), False)
        for k in writes:
            b = self.bufs.get(k)
            if b:
                if b["w"]:
                    add(b["w"], False)
                for sname, (sem, val) in b["r"].items():
                    add((sname, sem, val), False)
        for sname, (sem, val) in need.items():
            if E["known"].get(sname, 0) < val:
                E["e"].wait_ge(sem, val)
                E["known"][sname] = val
                self.ninstr += 1

    def _record(self, tok, reads, writes):
        sname, sem, val = tok
        for k in reads:
            b = self.bufs.setdefault(k, dict(w=None, r={}))
            b["r"][sname] = (sem, val)
        for k in writes:
            self.bufs[k] = dict(w=tok, r={})

    def op(self, en, fn, reads=(), writes=(), signal=True):
        E = self.E[en]
        self._wait(en, reads, writes)
        ins = fn(E["e"])
        self.ninstr += 1
        if signal:
            E["cnt"] += 1
            ins.then_inc(E["sem"], 1)
            tok = ("s_" + en, E["sem"], E["cnt"])
            for (r, w) in E["pend"]:
                self._record(tok, r, w)
            E["pend"] = []
            self._record(tok, reads, writes)
        else:
            E["pend"].append((tuple(reads), tuple(writes)))
            tok = ("s_" + en, E["sem"], E["cnt"] + 1)
            self._record(tok, reads, writes)

    def dma(self, out, in_, sem, reads=(), writes=(), q="sp"):
        E = self.E[q]
        self._wait(q, reads, writes)
        S = self.dsem(sem)
        ins = E["e"].dma_start(out=out, in_=in_)
        S["cnt"] += 16
        ins.then_inc(S["sem"], 16)
        self.ninstr += 1
        self._record(("d_" + sem, S["sem"], S["cnt"]), reads, writes)

    def barrier(self):
        for en, E in self.E.items():
            assert not E["pend"], en
        for en, E in self.E.items():
            for on, O in self.E.items():
                if on != en and O["cnt"] > E["known"].get("s_" + on, 0):
                    E["e"].wait_ge(O["sem"], O["cnt"])
                    E["known"]["s_" + on] = O["cnt"]
            for sn, S in self.ds.items():
                if S["cnt"] > E["known"].get("d_" + sn, 0):
                    E["e"].wait_ge(S["sem"], S["cnt"])
                    E["known"]["d_" + sn] = S["cnt"]
        self.bufs = {}


class Prog:
    def __init__(self, T, depth, dbg=()):
        self.T = T
        self.depth = depth
        self.dbg = set(dbg)
        self.NT = T // 512
        self.NB = T // 128
        self.uid = 0

    def nm(self, n):
        return f"{n}_{self.uid}"

    def dram(self, name, shape, dt, kind="Internal"):
        if name in self.dbg:
            kind = "ExternalOutput"
        return self.nc.dram_tensor(name, list(shape), dt, kind=kind).ap()

    def build(self):
        T, depth = self.T, self.depth
        nc = bass.Bass("TRN2", target_bir_lowering=False)
        self.nc = nc
        I = lambda n, s, dt=F32: nc.dram_tensor(n, list(s), dt, kind="ExternalInput").ap()
        self.x_in = I("x", [T, D])
        self.w_in = I("w_in", [depth, D, IN_COLS])
        self.norm_w = I("norm_w8", [depth, 128, 8])
        self.conv = I("conv_l", [depth, 128, 24 * 5])
        self.alog = I("alog_r", [depth, 128, 16])
        self.dtb = I("dtb_r", [depth, 128, 16])
        self.anw = I("anw_c", [depth, 128, 1])
        self.qnw_c = I("qnw_c", [depth, 128, 1])
        self.knw_c = I("knw_c", [depth, 128, 1])
        self.qnw_r = I("qnw_r", [depth, 1, 64])
        self.knw_r = I("knw_r", [depth, 1, 64])
        self.w_a = I("w_a_out", [depth, 1024, 1024])
        self.w_b = I("w_b_out", [depth, 512, 1024])
        self.w_o = I("w_out", [depth, 1024, 1024])
        self.c_f32 = I("c_f32", [128, NCF * 128])
        self.c_bf = I("c_bf", [128, NCB * 128], BF16)
        self.c_sel = I("c_sel", [8, 1024])
        self.c_sel12 = I("c_sel12", [12, 768])
        self.c_eps = I("c_eps", [128, 4])
        self.c_cos = I("c_cos", [128, T])
        self.c_sin = I("c_sin", [128, T])
        self.fl = I("flags", [128, 2])
        self.y = nc.dram_tensor("y", [T, D], F32, kind="ExternalOutput").ap()
        self.W1 = self.dram("W1", [depth, NCH, 128, 1024], BF16)
        self.WA = self.dram("WA", [depth, 128, 8, 1024], BF16)
        self.WB = self.dram("WB", [depth, 128, 4, 1024], BF16)
        self.WO = self.dram("WO", [depth, 128, 8, 1024], BF16)
        self.QA = self.dram("QA", [3072, T], BF16)
        self.ZA = self.dram("ZA", [1024, T], BF16)
        self.GA = self.dram("GA", [T, 32], F32)
        self.QKB = self.dram("QKB", [3072, T], BF16)
        self.VB = self.dram("VB", [T, 1536], BF16)
        self.ZB = self.dram("ZB", [512, T], BF16)
        self.MG = self.dram("MG", [2048, T], BF16)
        self.OD = [self.dram("OF", [1024, T], BF16), self.dram("OB", [1024, T], BF16)]
        self.OGB = self.dram("OGB", [512, T], BF16)
        self.QF = [self.dram("QFq", [1024, T], BF16), self.dram("QFk", [1024, T], BF16)]
        self.TM = [self.dram("TMk", [T, 1024], BF16), self.dram("TMv", [T, 1024], BF16)]
        self.XS = [self.dram("XS0", [T, D], F32), self.dram("XS1", [T, D], F32)]

        with ExitStack() as gst:
            self.tk = TK(nc, gst)
            tk = self.tk
            sb = lambda n, s, dt: gst.enter_context(nc.sbuf_tensor(self.nm(n), list(s), dt))
            self.cf = sb("cf", [128, NCF * 128], F32)
            self.cb = sb("cb", [128, NCB * 128], BF16)
            self.csel = sb("csel", [8, 1024], F32)
            self.flg = sb("flg", [128, 2], F32)
            self.csel12 = sb("csel12", [12, 768], F32)
            self.epsc = sb("epsc", [128, 4], F32)
            self.onec = self.epsc[:, 3:4]
            tk.dma(self.csel12[:], self.c_sel12[:, :], "g0", writes=["csel12"])
            tk.dma(self.epsc[:], self.c_eps[:, :], "g0", writes=["epsc"])
            tk.dma(self.cf[:], self.c_f32[:, :], "g0", writes=["cf"])
            tk.dma(self.cb[:], self.c_bf[:, :], "g0", writes=["cb"])
            tk.dma(self.csel[:], self.c_sel[:, :], "g0", writes=["csel"])
            tk.dma(self.flg[:], self.fl[:, :], "g0", writes=["flg"])
            tk.barrier()
            for l in range(depth):
                self.pass0(l)
            tk.barrier()
            for l in range(depth):
                xin = self.x_in if l == 0 else self.XS[(l - 1) % 2]
                xout = self.y if l == depth - 1 else self.XS[l % 2]
                if "skip1" not in self.dbg:
                    self.pass1(l, xin)
                    tk.barrier()
                if "skip2" not in self.dbg:
                    self.pass2(l)
                    tk.barrier()
                if "skip3" not in self.dbg:
                    self.pass3(l)
                    tk.barrier()
                if "skip4" not in self.dbg:
                    self.pass4(l, xin, xout)
                    tk.barrier()
            tk.barrier()
        return nc

    def CF(self, i, n=1):
        return self.cf[:, i * 128:(i + n) * 128]

    def CB(self, i, n=1):
        return self.cb[:, i * 128:(i + n) * 128]

    def pass0(self, l):
        self.uid += 1
        nc, tk = self.nc, self.tk
        with ExitStack() as st:
            sb = lambda n, s, dt: st.enter_context(nc.sbuf_tensor(self.nm(n), list(s), dt))
            wf = [sb(f"p0f{i}", [128, 1024], F32) for i in range(3)]
            wb = [sb(f"p0b{i}", [128, 1024], BF16) for i in range(3)]
            nw = sb("p0nw", [128, 8], F32)
            an = sb("p0an", [128, 1], F32)
            tk.dma(nw[:], self.norm_w[l], "p0s", writes=["p0nw"])
            tk.dma(an[:], self.anw[l], "p0s", writes=["p0an"])
            half = sb("p0half", [128, 1], F32)
            tk.op("dve", lambda e: e.tensor_scalar(an[:], an[:], 0.5, None, ALU.mult), ["p0an"], ["p0an"])
            tk.op("pool", lambda e: e.memset(half[:], 0.5), [], ["p0an"])
            cnt = [0]
            engs = ["dve", "act", "dve"]

            wf4 = [sb(f"p0g{i}", [128, 1024], F32) for i in range(4)]
            stgw = [sb(f"p0s{i}", [128, 8, 8, 128], BF16) for i in range(2)]
            jcnt = 0
            for gi_, grp in enumerate(WGROUPS):
                c0 = CHUNKS[grp[0]][0]
                ncol = sum(CHUNKS[c][1] for c in grp)
                g = len(grp)
                sgi = gi_ % 2
                sg = stgw[sgi]
                sgk = f"p0sg{sgi}"
                if g == 1:
                    tk.op("pool", lambda e: e.memset(sg[:, 0, :, :], 0.0), [], [sgk])
                for kc in range(8):
                    i = jcnt % 4
                    jcnt += 1
                    tk.dma(wf4[i][:, 0:ncol], self.w_in[l, kc * 128:(kc + 1) * 128, c0:c0 + ncol], f"p0m{i}", writes=[f"p0g{i}"])
                    sc = nw[:, kc:kc + 1]
                    if g == 1:
                        dst_ap = sg[:, 0, kc, 0:ncol]
                        src_ap = wf4[i][:, 0:ncol]
                    else:
                        dst_ap = sg[:, 0:g, kc, :]
                        src_ap = wf4[i][:, 0:ncol].rearrange("p (g j) -> p g j", j=128)
                    if i % 2 == 0:
                        tk.op("act", lambda e: e.activation(out=dst_ap, in_=src_ap, func=AF.Copy, scale=sc), [f"p0g{i}", "p0nw"], [sgk])
                    else:
                        tk.op("dve", lambda e: e.tensor_scalar(dst_ap, src_ap, sc, None, ALU.mult), [f"p0g{i}", "p0nw"], [sgk])
                tk.dma(self.W1[l, grp[0]:grp[0] + g].rearrange("g p j -> p g j"), sg[:, 0:g].rearrange("p g k j -> p g (k j)"), f"p0u{sgi}", reads=[sgk])
            for (wsrc, wdst, nk, scale) in ((self.w_a, self.WA, 8, an), (self.w_b, self.WB, 4, half), (self.w_o, self.WO, 8, half)):
                for kc in range(nk):
                    i = cnt[0] % 3
                    cnt[0] += 1
                    tk.dma(wf[i][:, :], wsrc[l, kc * 128:(kc + 1) * 128, :], f"p0l{i}", writes=[f"p0f{i}"])
                    en = engs[i]
                    rd = [f"p0f{i}", "p0an"]
                    if scale is not None:
                        sc = scale[:, 0:1]
                        if en == "act":
                            tk.op(en, lambda e: e.activation(out=wb[i][:, :], in_=wf[i][:, :], func=AF.Copy, scale=sc), rd, [f"p0b{i}"])
                        else:
                            tk.op(en, lambda e: e.tensor_scalar(wb[i][:, :], wf[i][:, :], sc, None, ALU.mult), rd, [f"p0b{i}"])
                    else:
                        if en == "act":
                            tk.op(en, lambda e: e.activation(out=wb[i][:, :], in_=wf[i][:, :], func=AF.Copy), rd, [f"p0b{i}"])
                        else:
                            tk.op(en, lambda e: e.tensor_copy(out=wb[i][:, :], in_=wf[i][:, :]), rd, [f"p0b{i}"])
                    tk.dma(wdst[l, :, kc, :], wb[i][:, :], f"p0t{i}", reads=[f"p0b{i}"])
            tk.barrier()

    def pass1(self, l, xin):
        self.uid += 1
        nc, tk, T = self.nc, self.tk, self.T
        NSUP = min(4, self.NT)
        NST = self.NT // NSUP
        with ExitStack() as st:
            sb = lambda n, s, dt: st.enter_context(nc.sbuf_tensor(self.nm(n), list(s), dt))
            ps = lambda n, s, dt: st.enter_context(nc.psum_tensor(self.nm(n), list(s), dt))
            xt = [sb(f"p1x{i}", [128, 4, 1024], F32) for i in range(2)]
            hb = [sb(f"p1h{i}", [128, 1024], BF16) for i in range(2)]
            junk = sb("p1junk", [128, 1024], BF16)
            ss = [sb(f"p1ss{i}", [128, 4], F32) for i in range(2)]
            rs = [sb(f"p1rs{i}", [128, 4], F32) for i in range(2)]
            hT = [sb(f"p1hT{i}", [128, 8, 512 * NSUP], BF16) for i in range(2)]
            wsb = [sb(f"p1w{i}", [128, 8, 1024], BF16) for i in range(3)]
            stg = [sb(f"p1st{i}", [128, 4, 512], BF16) for i in range(4)]
            thb = [sb(f"p1th{i}", [128, 512], F32) for i in range(2)]
            vst = [sb(f"p1vs{i}", [128, 512], BF16) for i in range(3)]
            gpre = sb("p1gp", [128, 4, 32], F32)
            gsp = sb("p1gs", [128, 4, 32], F32)
            gout = [sb(f"p1go{i}", [128, 4, 32], F32) for i in range(2)]
            nea = sb("p1nea", [128, 16], F32)
            dtb = sb("p1dtb", [128, 16], F32)
            ptr = ps("p1ptr", [128, 8, 128], BF16)
            pmm = [ps(f"p1pm{i}", [128, 512], F32) for i in range(6)]
            pg = ps("p1pg", [128, 4, 32], F32)
            NPM = 6
            tk.dma(nea[:], self.alog[l], "p1s", writes=["nea"])
            tk.dma(dtb[:], self.dtb[l], "p1s", writes=["dtb"])
            tk.op("act", lambda e: e.activation(out=nea[:], in_=nea[:], func=AF.Exp), ["nea"], ["nea"])
            tk.op("dve", lambda e: e.tensor_scalar(nea[:], nea[:], -1.0, None, ALU.mult), ["nea"], ["nea"])
            identb = self.CB(CB_ID)
            pmi, sti, vsi, thi, xcnt, gcnt = [0], [0], [0], [0], [0], [0]
            groups = [WGROUPS[4]] + WGROUPS[:4] + WGROUPS[5:]
            NG = len(groups)

            def load_x(sti_, sub):
                t0 = (sti_ * NSUP + sub) * 512
                xi = (sti_ * NSUP + sub) % 2
                tk.dma(xt[xi][:], xin[t0:t0 + 512, :].rearrange("(s p) d -> p s d", p=128), f"p1x{xi}", writes=[f"xt{xi}"])

            def norm_sub(sti_, sub):
                xi = (sti_ * NSUP + sub) % 2
                hs = sti_ % 2
                for s in range(4):
                    tk.op("act", lambda e: e.activation(out=junk[:], in_=xt[xi][:, s, :], func=AF.Square, scale=1.0 / 32.0,
                                                        accum_out=ss[xi][:, s:s + 1]), [f"xt{xi}"], ["junk", f"ss{xi}"])
                tk.op("act", lambda e: e.activation(out=rs[xi][:], in_=ss[xi][:], func=AF.Ln, bias=self.epsc[:, 0:1]), [f"ss{xi}"], [f"rs{xi}"])
                tk.op("act", lambda e: e.activation(out=rs[xi][:], in_=rs[xi][:], func=AF.Exp, scale=-0.5), [f"rs{xi}"], [f"rs{xi}"])
                for s in range(4):
                    h = hb[s % 2]
                    hk = f"hb{s % 2}"
                    tk.op("act", lambda e: e.activation(out=h[:], in_=xt[xi][:, s, :], func=AF.Copy, scale=rs[xi][:, s:s + 1]),
                          [f"xt{xi}", f"rs{xi}"], [hk])
                    for kc in range(8):
                        tk.op("pe", lambda e: e.transpose(ptr[:, kc, :], h[:, kc * 128:(kc + 1) * 128], identb), [hk, "cb"], ["ptr"], signal=(kc == 7))
                    c0 = sub * 512 + s * 128
                    tk.op("dve", lambda e: e.tensor_copy(out=hT[hs][:, :, c0:c0 + 128], in_=ptr[:]), ["ptr"], [f"hT{hs}_{sub}"])

            jobs = [(sti_, gi) for sti_ in range(NST) for gi in range(NG)]

            def load_w(ji):
                sti_, gi = jobs[ji]
                grp = groups[gi]
                wslot = ji % 3
                g = len(grp)
                tk.dma(wsb[wslot][:, 0:g, :], self.W1[l, grp[0]:grp[0] + g].rearrange("g p j -> p g j"), f"p1w{wslot}", writes=[f"w{wslot}"])

            load_x(0, 0)
            if NSUP > 1:
                load_x(0, 1)
            load_w(0)
            load_w(1)
            for sub in range(NSUP):
                norm_sub(0, sub)
                if sub + 2 < NSUP:
                    load_x(0, sub + 2)
            for ji, (sti_, gi) in enumerate(jobs):
                if ji + 2 < len(jobs):
                    load_w(ji + 2)
                if sti_ + 1 < NST:
                    if gi == 0:
                        load_x(sti_ + 1, 0)
                        if NSUP > 1:
                            load_x(sti_ + 1, 1)
                    if 1 <= gi <= NSUP:
                        sub = gi - 1
                        norm_sub(sti_ + 1, sub)
                        if sub + 2 < NSUP:
                            load_x(sti_ + 1, sub + 2)
                grp = groups[gi]
                g = len(grp)
                wslot = ji % 3
                w = wsb[wslot]
                wk = f"w{wslot}"
                hs = sti_ % 2
                hTt = hT[hs]
                kind = CHUNKS[grp[0]][2]
                for sub in range(NSUP):
                    t0 = (sti_ * NSUP + sub) * 512
                    hTk = f"hT{hs}_{sub}"
                    sc = slice(sub * 512, sub * 512 + 512)
                    if kind in (SEC_QKVA, SEC_ZA, SEC_QKB, SEC_ZB, SEC_MG):
                        dst = {SEC_QKVA: self.QA, SEC_ZA: self.ZA, SEC_QKB: self.QKB, SEC_ZB: self.ZB, SEC_MG: self.MG}[kind]
                        for q0 in range(0, g, 4):
                            sslot = sti[0] % 4
                            sti[0] += 1
                            sg = stg[sslot]
                            for gi_ in range(q0, q0 + 4):
                                p = pmm[pmi[0] % NPM]
                                pk = f"pm{pmi[0] % NPM}"
                                pmi[0] += 1
                                for kc in range(8):
                                    tk.op("pe", lambda e: e.matmul(p[:], lhsT=w[:, gi_, kc * 128:(kc + 1) * 128], rhs=hTt[:, kc, sc],
                                                                   start=(kc == 0), stop=(kc == 7)), [wk, hTk], [pk], signal=(kc == 7))
                                if kind in (SEC_QKVA, SEC_QKB):
                                    if pmi[0] % 2 == 0:
                                        tk.op("act", lambda e: e.activation(out=sg[:, gi_ - q0, :], in_=p[:], func=AF.Copy), [pk], [f"stg{sslot}"])
                                    else:
                                        tk.op("dve", lambda e: e.tensor_copy(out=sg[:, gi_ - q0, :], in_=p[:]), [pk], [f"stg{sslot}"])
                                elif kind == SEC_MG:
                                    tk.op("act", lambda e: e.activation(out=sg[:, gi_ - q0, :], in_=p[:], func=AF.Tanh, scale=0.5), [pk], [f"stg{sslot}"])
                                else:
                                    th = thb[thi[0] % 2]
                                    thk = f"th{thi[0] % 2}"
                                    thi[0] += 1
                                    tk.op("act", lambda e: e.activation(out=th[:], in_=p[:], func=AF.Tanh, scale=0.5), [pk], [thk])
                                    tk.op("dve", lambda e: e.scalar_tensor_tensor(sg[:, gi_ - q0, :], th[:], 1.0, p[:], ALU.add, ALU.mult), [pk, thk], [f"stg{sslot}"])
                            r0 = CHUNKS[grp[q0]][3] * 128
                            tk.dma(dst[r0:r0 + 512, t0:t0 + 512].rearrange("(g p) t -> p g t", p=128), sg[:], f"p1st{sslot}", reads=[f"stg{sslot}"])
                    elif kind == SEC_VB:
                        vq = CHUNKS[grp[0]][3] // 4
                        for s in range(4):
                            p = pmm[pmi[0] % NPM]
                            pk = f"pm{pmi[0] % NPM}"
                            pmi[0] += 1
                            for kc in range(8):
                                tk.op("pe", lambda e: e.matmul(p[:], lhsT=hTt[:, kc, sub * 512 + s * 128:sub * 512 + (s + 1) * 128], rhs=w[:, 0:4, kc * 128:(kc + 1) * 128],
                                                               start=(kc == 0), stop=(kc == 7)), [wk, hTk], [pk], signal=(kc == 7))
                            vslot = vsi[0] % 3
                            vsi[0] += 1
                            if vsi[0] % 2 == 0:
                                tk.op("act", lambda e: e.activation(out=vst[vslot][:], in_=p[:], func=AF.Copy), [pk], [f"vst{vslot}"])
                            else:
                                tk.op("dve", lambda e: e.tensor_copy(out=vst[vslot][:], in_=p[:]), [pk], [f"vst{vslot}"])
                            tk.dma(self.VB[t0 + s * 128:t0 + (s + 1) * 128, vq * 512:(vq + 1) * 512], vst[vslot][:], f"p1vs{vslot}", reads=[f"vst{vslot}"])
                    else:
                        gs_ = gcnt[0] % 2
                        gcnt[0] += 1
                        go = gout[gs_]
                        gk = f"go{gs_}"
                        for s in range(4):
                            for kc in range(8):
                                tk.op("pe", lambda e: e.matmul(pg[:, s, :], lhsT=hTt[:, kc, sub * 512 + s * 128:sub * 512 + (s + 1) * 128], rhs=w[:, 0, kc * 128:kc * 128 + 32],
                                                               start=(kc == 0), stop=(kc == 7)), [wk, hTk], ["pg"], signal=(kc == 7 and s == 3))
                        dtb_b = dtb[:, :].unsqueeze(1).broadcast_to([128, 4, 16])
                        nea_b = nea[:, :].unsqueeze(1).broadcast_to([128, 4, 16])
                        tk.op("dve", lambda e: e.tensor_tensor(out=gpre[:, :, 0:16], in0=pg[:, :, 0:16], in1=dtb_b, op=ALU.add), ["pg", "dtb"], ["gpre"])
                        tk.op("dve", lambda e: e.tensor_scalar(gpre[:, :, 16:32], pg[:, :, 16:32], -1.0, None, ALU.mult), ["pg"], ["gpre"])
                        tk.op("act", lambda e: e.activation(out=gsp[:], in_=gpre[:], func=AF.Exp), ["gpre"], ["gsp"])
                        tk.op("act", lambda e: e.activation(out=gsp[:], in_=gsp[:], func=AF.Ln, bias=self.onec[:, 0:1]), ["gsp"], ["gsp"])
                        tk.op("dve", lambda e: e.tensor_tensor(out=go[:, :, 0:16], in0=gsp[:, :, 0:16], in1=nea_b, op=ALU.mult), ["gsp", "nea"], [gk])
                        tk.op("dve", lambda e: e.tensor_scalar(go[:, :, 16:32], gsp[:, :, 16:32], -1.0, None, ALU.mult), ["gsp"], [gk])
                        tk.dma(self.GA[t0:t0 + 512, :].rearrange("(s p) c -> p s c", p=128), go[:], f"p1go{gs_}", reads=[gk])
            tk.barrier()

    def pass2(self, l):
        self.pass2a(l)
        self.tk.barrier()
        if "only2a" not in self.dbg:
            self.pass2b(l)

    def pass2a(self, l):
        self.uid += 1
        nc, tk, T = self.nc, self.tk, self.T
        with ExitStack() as st:
            sb = lambda n, s, dt: st.enter_context(nc.sbuf_tensor(self.nm(n), list(s), dt))
            ps = lambda n, s, dt: st.enter_context(nc.psum_tensor(self.nm(n), list(s), dt))
            cw = sb("a2cw", [128, 120], F32)
            dg = sb("a2dg", [128, 120, 128], BF16)
            raw = [sb(f"a2raw{i}", [128, 8, 516], BF16) for i in range(3)]
            out = [sb(f"a2out{i}", [128, 8, 512], BF16) for i in range(3)]
            th = [sb(f"a2th{i}", [128, 512], F32) for i in range(3)]
            sq = [sb(f"a2sq{i}", [128, 512], BF16) for i in range(2)]
            lnr = sb("a2lnr", [8, 2, 512], F32)
            rs8 = sb("a2rs8", [8, 2, 512], F32)
            tm = [sb(f"a2tm{i}", [128, 8, 128], BF16) for i in range(4)]
            PP = [ps(f"a2PP{i}", [128, 512], F32) for i in range(3)]
            PR = [ps(f"a2PR{i}", [128, 512], F32) for i in range(2)]
            PT = [ps(f"a2PT{i}", [128, 8, 128], BF16) for i in range(2)]
            identb = self.CB(CB_ID)
            oh8 = self.cb[:, CB_OH8 * 128:CB_OH8 * 128 + 64].rearrange("p (h m) -> p h m", m=8)
            tk.dma(cw[:], self.conv[l], "a2s", writes=["cw"])
            for c in range(120):
                tk.op("pool", lambda e: e.tensor_scalar(dg[:, c, :], identb, cw[:, c:c + 1], None, ALU.mult), ["cw", "cb"], ["dg"])
            jobs = [(tt, which) for tt in range(self.NT) for which in range(3)]

            def load_raw(ji):
                tt, which = jobs[ji]
                t0 = tt * 512
                r = raw[ji % 3]
                rk = f"raw{ji % 3}"
                lo, hi = max(t0 - 2, 0), min(t0 + 514, T)
                if t0 == 0:
                    tk.op("pool", lambda e: e.memset(r[:, :, 0:2], 0.0), [], [rk])
                if t0 + 512 == T:
                    tk.op("pool", lambda e: e.memset(r[:, :, 514:516], 0.0), [], [rk])
                tk.dma(r[:, :, lo - (t0 - 2):hi - (t0 - 2)],
                       self.QA[which * 1024:(which + 1) * 1024, lo:hi].rearrange("(h p) t -> p h t", p=128), f"a2raw{ji % 3}", writes=[rk])
                if t0 + 512 == T // 2:
                    tk.op("pool", lambda e: e.tensor_scalar(r[:, :, 514:516], r[:, :, 514:516], self.flg[:, 0:1], None, ALU.mult), [rk, "flg"], [rk])
                if t0 == T // 2:
                    tk.op("pool", lambda e: e.tensor_scalar(r[:, :, 0:2], r[:, :, 0:2], self.flg[:, 0:1], None, ALU.mult), [rk, "flg"], [rk])

            load_raw(0)
            load_raw(1)
            ppi, tmi, pti = [0], [0], [0]
            for tt in range(self.NT):
                t0 = tt * 512
                for which in range(3):
                    ji = tt * 3 + which
                    if ji + 2 < len(jobs):
                        load_raw(ji + 2)
                    r = raw[ji % 3]
                    rk = f"raw{ji % 3}"
                    dst = out[which]
                    dk_ = f"out{which}"
                    for h in range(8):
                        c = which * 8 + h
                        pj = ppi[0] % 3
                        ppi[0] += 1
                        pp, ppk = PP[pj], f"PP{pj}"
                        for j in range(5):
                            tk.op("pe", lambda e: e.matmul(pp[:], lhsT=dg[:, c * 5 + j, :], rhs=r[:, h, j:j + 512], start=(j == 0), stop=(j == 4)),
                                  ["dg", rk], [ppk], signal=(j == 4))
                        tk.op("act", lambda e: e.activation(out=th[pj][:], in_=pp[:], func=AF.Tanh, scale=0.5), [ppk], [f"th{pj}"])
                        tk.op("dve", lambda e: e.scalar_tensor_tensor(dst[:, h, :], th[pj][:], 1.0, pp[:], ALU.add, ALU.mult), [ppk, f"th{pj}"], [dk_])
                for which in range(2):
                    dst, dk_ = out[which], f"out{which}"
                    for h in range(8):
                        tk.op("act", lambda e: e.activation(out=sq[h % 2][:], in_=dst[:, h, :], func=AF.Square), [dk_], [f"sq{h % 2}"])
                        tk.op("pe", lambda e: e.matmul(PR[which][0:8, :], lhsT=oh8[:, h, :], rhs=sq[h % 2][:], start=(h == 0), stop=(h == 7)),
                              [f"sq{h % 2}", "cb"], [f"PR{which}"])
                tk.op("act", lambda e: e.activation(out=lnr[:, 0, :], in_=PR[0][0:8, :], func=AF.Ln, scale=128.0, bias=self.epsc[0:8, 1:2]), ["PR0"], ["lnr"])
                tk.op("act", lambda e: e.activation(out=lnr[:, 1, :], in_=PR[1][0:8, :], func=AF.Ln, bias=self.epsc[0:8, 2:3]), ["PR1"], ["lnr"])
                tk.op("act", lambda e: e.activation(out=rs8[:], in_=lnr[:], func=AF.Exp, scale=-0.5), ["lnr"], ["rs8"])
                for which in range(2):
                    dst, dk_ = out[which], f"out{which}"
                    for h in range(8):
                        pj = ppi[0] % 3
                        ppi[0] += 1
                        pp, ppk = PP[pj], f"PP{pj}"
                        tk.op("pe", lambda e: e.matmul(pp[:], lhsT=self.csel[:, h * 128:(h + 1) * 128], rhs=rs8[:, which, :], start=True, stop=True), ["rs8", "csel"], [ppk])
                        tk.op("dve", lambda e: e.tensor_tensor(out=dst[:, h, :], in0=dst[:, h, :], in1=pp[:], op=ALU.mult), [ppk, dk_], [dk_])
                    tk.dma(self.QF[which][:, t0:t0 + 512].rearrange("(h p) t -> p h t", p=128), dst[:], f"a2o{which}", reads=[dk_])
                for which in (1, 2):
                    dst, dk_ = out[which], f"out{which}"
                    for bi in range(4):
                        pt_i = pti[0] % 2
                        pti[0] += 1
                        for h in range(8):
                            tk.op("pe", lambda e: e.transpose(PT[pt_i][:, h, :], dst[:, h, bi * 128:(bi + 1) * 128], identb), [dk_, "cb"], [f"PT{pt_i}"], signal=(h == 7))
                        ti_ = tmi[0] % 4
                        tmi[0] += 1
                        if ti_ % 2 == 0:
                            tk.op("act", lambda e: e.activation(out=tm[ti_][:], in_=PT[pt_i][:], func=AF.Copy), [f"PT{pt_i}"], [f"tm{ti_}"])
                        else:
                            tk.op("dve", lambda e: e.tensor_copy(out=tm[ti_][:], in_=PT[pt_i][:]), [f"PT{pt_i}"], [f"tm{ti_}"])
                        tk.dma(self.TM[which - 1][t0 + bi * 128:t0 + (bi + 1) * 128, :], tm[ti_][:].rearrange("p h d -> p (h d)"), f"a2t{ti_}", reads=[f"tm{ti_}"])
            tk.barrier()

    def pass2b(self, l):
        self.uid += 1
        nc, tk, T = self.nc, self.tk, self.T
        with ExitStack() as st:
            sb = lambda n, s, dt: st.enter_context(nc.sbuf_tensor(self.nm(n), list(s), dt))
            ps = lambda n, s, dt: st.enter_context(nc.psum_tensor(self.nm(n), list(s), dt))
            identb, identf, onesf = self.CB(CB_ID), self.CF(CF_ID), self.CF(CF_ONES)
            c3 = lambda blk: self.cb[:, blk * 128:(blk + 8) * 128].rearrange("p (h i) -> p h i", i=128)
            idr, bd32, off64, off128 = c3(CB_IDR), c3(CB_BD32), c3(CB_OFF64), c3(CB_OFF128)
            SLOTS = ["qg", "PTn", "Vb", "Ao64", "Ao128", "Ya", "Yb", "Ma", "Mb", "Aa", "Ab", "X", "Z", "Rn", "nvb", "nvd"]
            ALIAS = {"T1": "Z", "Ds": "X", "Dq": "Rn", "M0": "nvb", "A": "nvd", "A32x": "Ab", "M": "Mb", "M32": "Mb"}
            NB = self.NB
            HS = [slice(0, 4), slice(4, 8)]

            def stream(dr):
                sid = f"s{dr}"
                K = lambda n, hf: sid + ALIAS.get(n, n) + "ab"[hf]
                Hs = {n: sb(f"b2{sid}{n}", [128, 8, 128], BF16) for n in SLOTS}
                H = lambda n, hf: Hs[ALIAS.get(n, n)][:, HS[hf], :]
                H1 = lambda n, h: Hs[ALIAS.get(n, n)][:, h, :]
                qf = [sb(f"b2{sid}qf{i}", [128, 8, 128], BF16) for i in range(2)]
                kf = [sb(f"b2{sid}kf{i}", [128, 8, 128], BF16) for i in range(2)]
                ktm = [sb(f"b2{sid}kt{i}", [128, 8, 128], BF16) for i in range(2)]
                vtm = [sb(f"b2{sid}vt{i}", [128, 8, 128], BF16) for i in range(2)]
                ga = [sb(f"b2{sid}ga{i}", [128, 32], F32) for i in range(2)]
                rhsD = sb(f"b2{sid}rhsD", [128, 8, 128], F32)
                ngc = sb(f"b2{sid}ngc", [8, 128], F32)
                ex = sb(f"b2{sid}ex", [128, 5, 8], F32)
                S32 = sb(f"b2{sid}S32", [128, 8, 128], F32)
                Sb = sb(f"b2{sid}Sb", [128, 8, 128], BF16)
                tq = [sb(f"b2{sid}tq{i}", [128, 4, 128], F32) for i in range(2)]
                tq2 = [sb(f"b2{sid}tq2{i}", [128, 4, 128], F32) for i in range(2)]
                ost = [sb(f"b2{sid}ost{i}", [128, 8, 512], BF16) for i in range(2)]
                PH = ps(f"b2{sid}PH", [128, 8, 128], F32)
                PT = ps(f"b2{sid}PT", [128, 8, 128], BF16)
                PS = ps(f"b2{sid}PS", [128, 512], F32)
                PHA = [sid + "PHa", sid + "PHb"]
                PTK = [sid + "PT"]
                PSK = sid + "PS"
                P0 = PH[:].rearrange("p h i -> p (h i)")
                tri = self.CF(CF_TRI0 + dr)
                trc = self.CF(CF_TRC0 + dr)
                mb = self.cb[:, (CB_MB0 + 8 * dr) * 128:(CB_MB0 + 8 * dr + 8) * 128]
                order = list(range(NB)) if dr == 0 else list(range(NB - 1, -1, -1))

                def load_blk(n):
                    gb = order[n]
                    i = n % 2
                    b0 = gb * 128
                    tk.dma(qf[i][:], self.QF[0][:, b0:b0 + 128].rearrange("(h p) t -> p h t", p=128), f"{sid}lq{i}", writes=[f"{sid}qf{i}"])
                    tk.dma(kf[i][:], self.QF[1][:, b0:b0 + 128].rearrange("(h p) t -> p h t", p=128), f"{sid}lq{i}", writes=[f"{sid}kf{i}"])
                    tk.dma(ktm[i][:].rearrange("p h d -> p (h d)"), self.TM[0][b0:b0 + 128, :], f"{sid}lt{i}", writes=[f"{sid}kt{i}"])
                    tk.dma(vtm[i][:].rearrange("p h d -> p (h d)"), self.TM[1][b0:b0 + 128, :], f"{sid}lt{i}", writes=[f"{sid}vt{i}"])
                    tk.dma(ga[i][:], self.GA[b0:b0 + 128, :], f"{sid}lg{i}", writes=[f"{sid}ga{i}"])

                def hmm(Ln, Rn_, lhs_fn=None, rhs_fn=None, extra_reads=(), add=None):
                    for hf in range(2):
                        rd = list(extra_reads)
                        if Ln:
                            rd.append(K(Ln, hf))
                        if Rn_:
                            rd.append(K(Rn_, hf))
                        if add:
                            rd += [K(add, hf), "cb"]
                        for h in range(4 * hf, 4 * hf + 4):
                            lh = lhs_fn(h) if lhs_fn else H1(Ln, h)
                            rh = rhs_fn(h) if rhs_fn else H1(Rn_, h)
                            if add:
                                tk.op("pe", lambda e: e.matmul(PH[:, h, :], lhsT=lh, rhs=rh, start=True, stop=False), rd, [PHA[hf]], signal=False)
                                tk.op("pe", lambda e: e.matmul(PH[:, h, :], lhsT=identb, rhs=H1(add, h), start=False, stop=True), rd, [PHA[hf]], signal=(h % 4 == 3))
                            else:
                                tk.op("pe", lambda e: e.matmul(PH[:, h, :], lhsT=lh, rhs=rh, start=True, stop=True), rd, [PHA[hf]], signal=(h % 4 == 3))

                def htr(src, reads_fn):
                    for hf in range(2):
                        for h in range(4 * hf, 4 * hf + 4):
                            tk.op("pe", lambda e: e.transpose(PT[:, h, :], src(h), identb), list(reads_fn(hf)) + ["cb"], PTK, signal=(h % 4 == 3))

                def cp(dst_, neg=False):
                    for hf in range(2):
                        if hf == 0:
                            tk.op("act", lambda e: e.activation(out=H(dst_, hf), in_=PH[:, HS[hf], :], func=AF.Copy, scale=(-1.0 if neg else 1.0)), [PHA[hf]], [K(dst_, hf)])
                        elif neg:
                            tk.op("dve", lambda e: e.tensor_scalar(H(dst_, hf), PH[:, HS[hf], :], -1.0, None, ALU.mult), [PHA[hf]], [K(dst_, hf)])
                        else:
                            tk.op("dve", lambda e: e.tensor_copy(out=H(dst_, hf), in_=PH[:, HS[hf], :]), [PHA[hf]], [K(dst_, hf)])

                def cpT(dst_, eng="act", mul=None):
                    for hf in range(2):
                        if mul is None:
                            tk.op("act", lambda e: e.activation(out=H(dst_, hf), in_=PT[:, HS[hf], :], func=AF.Copy), PTK, [K(dst_, hf)])
                        elif mul is None:
                            tk.op("dve", lambda e: e.tensor_copy(out=H(dst_, hf), in_=PT[:, HS[hf], :]), PTK, [K(dst_, hf)])
                        else:
                            tk.op("dve", lambda e: e.tensor_tensor(out=H(dst_, hf), in0=PT[:, HS[hf], :], in1=mul(hf), op=ALU.mult), PTK + [sid + "ex"], [K(dst_, hf)])

                def yadd(ysrc, ydst, sign=1.0):
                    for hf in range(2):
                        if sign > 0:
                            tk.op("dve", lambda e: e.tensor_tensor(out=H(ydst, hf), in0=PH[:, HS[hf], :], in1=H(ysrc, hf), op=ALU.add), [PHA[hf], K(ysrc, hf)], [K(ydst, hf)])
                        else:
                            tk.op("dve", lambda e: e.scalar_tensor_tensor(H(ydst, hf), PH[:, HS[hf], :], -1.0, H(ysrc, hf), ALU.mult, ALU.add), [PHA[hf], K(ysrc, hf)], [K(ydst, hf)])

                def pool2(dst_, a_, b_c, op, a_first=True):
                    for hf in range(2):
                        cs_ = b_c[:, HS[hf], :]
                        if a_first:
                            tk.op("pool", lambda e: e.tensor_tensor(out=H(dst_, hf), in0=H(a_, hf), in1=cs_, op=op), [K(a_, hf), "cb"], [K(dst_, hf)])
                        else:
                            tk.op("pool", lambda e: e.tensor_tensor(out=H(dst_, hf), in0=cs_, in1=H(a_, hf), op=op), [K(a_, hf), "cb"], [K(dst_, hf)])

                tk.op("pool", lambda e: e.memset(S32[:], 0.0), [], [sid + "S32a", sid + "S32b"])
                tk.op("pool", lambda e: e.memset(Sb[:], 0.0), [], [sid + "Sba", sid + "Sbb"])
                load_blk(0)
                yield
                for n in range(NB):
                    gb = order[n]
                    i = n % 2
                    if n + 1 < NB:
                        load_blk(n + 1)
                    qT, kT, Ktm, Vtm = qf[i], kf[i], ktm[i], vtm[i]
                    qk_, kk_, ktk_, vtk_, gak_ = f"{sid}qf{i}", f"{sid}kf{i}", f"{sid}kt{i}", f"{sid}vt{i}", f"{sid}ga{i}"
                    g_ = ga[i][:, dr * 8:dr * 8 + 8]
                    lnb = ga[i][:, 16 + dr * 8:16 + dr * 8 + 8]
                    bi = gb % 4
                    tt = gb // 4
                    oslot = tt % 2
                    bc = slice(bi * 128, bi * 128 + 128)
                    tk.op("dve", lambda e: e.tensor_tensor(out=rhsD[:], in0=tri.unsqueeze(1).broadcast_to([128, 8, 128]),
                                                           in1=g_.unsqueeze(2).broadcast_to([128, 8, 128]), op=ALU.mult), [gak_, "cf"], [sid + "rhsD"])
                    tk.op("pe", lambda e: e.matmul(PS[0:8, 0:128], lhsT=g_, rhs=tri, start=True, stop=True), [gak_, "cf"], [PSK])
                    yield
                    tk.op("act", lambda e: e.activation(out=ngc[:], in_=PS[0:8, 0:128], func=AF.Copy, scale=-1.0), [PSK], [sid + "ngc"])
                    smm = [(0, tri, g_, True, True), (8, tri, g_, True, False), (8, identf, lnb, False, True), (16, trc, g_, True, True),
                           (24, onesf, g_, True, True), (32, identf, lnb, True, True)]
                    for n_, (c0, L_, R_, s0, s1) in enumerate(smm):
                        tk.op("pe", lambda e: e.matmul(PS[:, 128 + c0:128 + c0 + 8], lhsT=L_, rhs=R_, start=s0, stop=s1), [gak_, "cf"], [PSK], signal=(n_ == 5))
                    yield
                    tk.op("act", lambda e: e.activation(out=ex[:].rearrange("p a b -> p (a b)"), in_=PS[:, 128:168], func=AF.Exp), [PSK], [sid + "ex"])
                    bb = lambda row, hf: ex[:, row, HS[hf]].unsqueeze(2).broadcast_to([128, 4, 128])
                    for hf in range(2):
                        tk.op("dve", lambda e: e.tensor_tensor(out=H("Vb", hf), in0=Vtm[:, HS[hf], :], in1=bb(4, hf), op=ALU.mult), [vtk_, sid + "ex"], [K("Vb", hf)])
                    rD = rhsD[:].rearrange("p h i -> p (h i)")
                    for hf in range(2):
                        tk.op("pe", lambda e: e.matmul(P0[:, hf * 512:(hf + 1) * 512], lhsT=onesf, rhs=rD[:, hf * 512:(hf + 1) * 512], start=True, stop=True),
                              [sid + "rhsD", "cf"], [PHA[hf]])
                    yield
                    for hf in range(2):
                        tk.op("act", lambda e: e.activation(out=H("T1", hf), in_=PH[:, HS[hf], :], func=AF.Exp), [PHA[hf]], [K("T1", hf)])
                        tk.op("pe", lambda e: e.matmul(P0[:, hf * 512:(hf + 1) * 512], lhsT=onesf, rhs=rD[:, hf * 512:(hf + 1) * 512], start=True, stop=False),
                              [sid + "rhsD", "cf"], [PHA[hf]], signal=False)
                        tk.op("pe", lambda e: e.matmul(P0[:, hf * 512:(hf + 1) * 512], lhsT=ngc[:], rhs=self.csel[:, hf * 512:(hf + 1) * 512], start=False, stop=False),
                              [sid + "ngc", "csel"], [PHA[hf]], signal=False)
                        tk.op("pe", lambda e: e.matmul(P0[:, hf * 512:(hf + 1) * 512], lhsT=identb, rhs=mb[:, hf * 512:(hf + 1) * 512], start=False, stop=True),
                              ["cb"], [PHA[hf]])
                    for hf in range(2):
                        tk.op("pool", lambda e: e.tensor_tensor(out=H("qg", hf), in0=qT[:, HS[hf], :], in1=H("T1", hf), op=ALU.mult), [qk_, K("T1", hf)], [K("qg", hf)])
                    yield
                    for hf in range(2):
                        tk.op("act", lambda e: e.activation(out=H("Ds", hf), in_=PH[:, HS[hf], :], func=AF.Exp), [PHA[hf]], [K("Ds", hf)])
                    pool2("Dq", "Ds", idr, ALU.add)
                    hmm(None, None, lambda h: kT[:, h, :], lambda h: kT[:, h, :], [kk_])
                    yield
                    for hf in range(2):
                        tk.op("dve", lambda e: e.tensor_tensor(out=H("M0", hf), in0=PH[:, HS[hf], :], in1=H("Ds", hf), op=ALU.mult), [PHA[hf], K("Ds", hf)], [K("M0", hf)])
                    hmm(None, None, lambda h: kT[:, h, :], lambda h: qT[:, h, :], [kk_, qk_])
                    yield
                    for hf in range(2):
                        tk.op("dve", lambda e: e.scalar_tensor_tensor(H("PTn", hf), PH[:, HS[hf], :], -1.0, H("Dq", hf), ALU.mult, ALU.mult), [PHA[hf], K("Dq", hf)], [K("PTn", hf)])
                    htr(lambda h: H1("M0", h), lambda hf: [K("M0", hf)])
                    yield
                    cpT("A", mul=lambda hf: bb(4, hf))
                    htr(lambda h: H1("A", h), lambda hf: [K("A", hf)])
                    yield
                    cpT("M")
                    pool2("A32x", "A", bd32, ALU.mult)
                    pool2("M32", "M", bd32, ALU.mult)
                    pool2("Ya", "M32", idr, ALU.subtract, a_first=False)
                    pool2("Ao64", "A", off64, ALU.mult)
                    pool2("Ao128", "A", off128, ALU.mult)
                    yield
                    Ac, Mc, Yc = "A32x", "M32", "Ya"
                    for (Mn, An, Yn) in [("Ma", "Aa", "Yb"), ("Mb", "Ab", "Ya"), ("Ma", "Aa", "Yb")]:
                        hmm(Ac, Mc); yield
                        cp(Mn); hmm(Mc, Ac); yield
                        cp(An); hmm(An, Yc); yield
                        yadd(Yc, Yn)
                        Ac, Mc, Yc = An, Mn, Yn
                    hmm(Mc, Ac); yield
                    cp("Ab"); hmm("Ab", Yc); yield
                    yadd(Yc, "Ya")
                    htr(lambda h: H1("Ya", h), lambda hf: [K("Ya", hf)]); yield
                    cpT("X")
                    hmm("Ao64", "Ya"); yield
                    cp("Z"); hmm("X", "Z"); yield
                    yadd("Ya", "Yb", -1.0)
                    htr(lambda h: H1("Yb", h), lambda hf: [K("Yb", hf)]); yield
                    cpT("X")
                    hmm("Ao128", "Yb"); yield
                    cp("Z"); hmm("X", "Z"); yield
                    yadd("Yb", "Ya", -1.0)
                    SK = [sid + "S32a", sid + "S32b"]
                    SBK = [sid + "Sba", sid + "Sbb"]
                    reset = (dr == 0 and gb == NB // 2 - 1) or (dr == 1 and gb == NB // 2)
                    for hf in range(2):
                        for h in range(4 * hf, 4 * hf + 4):
                            tk.op("pe", lambda e: e.matmul(PH[:, h, :], lhsT=kT[:, h, :], rhs=Sb[:, h, :], start=True, stop=True), [kk_, SBK[hf]], [PHA[hf]], signal=(h % 4 == 3))
                    yield
                    for hf in range(2):
                        tk.op("dve", lambda e: e.tensor_tensor(out=tq[hf][:], in0=PH[:, HS[hf], :], in1=bb(1, hf), op=ALU.mult), [PHA[hf], sid + "ex"], [f"{sid}tq{hf}"])
                        tk.op("pool", lambda e: e.tensor_tensor(out=H("Rn", hf), in0=tq[hf][:], in1=H("Vb", hf), op=ALU.subtract), [f"{sid}tq{hf}", K("Vb", hf)], [K("Rn", hf)])
                        for h in range(4 * hf, 4 * hf + 4):
                            tk.op("pe", lambda e: e.matmul(PH[:, h, :], lhsT=H1("Ya", h), rhs=H1("Rn", h), start=True, stop=True), [K("Ya", hf), K("Rn", hf)], [PHA[hf]], signal=(h % 4 == 3))
                    yield
                    for hf in range(2):
                        tk.op("act", lambda e: e.activation(out=H("nvb", hf), in_=PH[:, HS[hf], :], func=AF.Copy), [PHA[hf]], [K("nvb", hf)])
                        tk.op("dve", lambda e: e.tensor_tensor(out=H("nvd", hf), in0=PH[:, HS[hf], :], in1=bb(2, hf), op=ALU.mult), [PHA[hf], sid + "ex"], [K("nvd", hf)])
                        for h in range(4 * hf, 4 * hf + 4):
                            tk.op("pe", lambda e: e.matmul(PH[:, h, :], lhsT=H1("nvb", h), rhs=H1("PTn", h), start=True, stop=False), [K("nvb", hf), K("PTn", hf)], [PHA[hf]], signal=False)
                            tk.op("pe", lambda e: e.matmul(PH[:, h, :], lhsT=Sb[:, h, :], rhs=H1("qg", h), start=False, stop=True), [SBK[hf], K("qg", hf)], [PHA[hf]], signal=(h % 4 == 3))
                    yield
                    for hf in range(2):
                        tk.op("act", lambda e: e.activation(out=ost[oslot][:, HS[hf], bc], in_=PH[:, HS[hf], :], func=AF.Copy), [PHA[hf]], [f"{sid}ost{oslot}"])
                        for h in range(4 * hf, 4 * hf + 4):
                            tk.op("pe", lambda e: e.matmul(PH[:, h, :], lhsT=Ktm[:, h, :], rhs=H1("nvd", h), start=True, stop=True), [ktk_, K("nvd", hf)], [PHA[hf]], signal=(h % 4 == 3))
                    yield
                    for hf in range(2):
                        tk.op("pool", lambda e: e.tensor_tensor(out=tq2[hf][:], in0=S32[:, HS[hf], :], in1=bb(3, hf), op=ALU.mult), [SK[hf], sid + "ex"], [f"{sid}tq2{hf}"])
                        tk.op("dve", lambda e: e.tensor_tensor(out=S32[:, HS[hf], :], in0=tq2[hf][:], in1=PH[:, HS[hf], :], op=ALU.subtract), [f"{sid}tq2{hf}", PHA[hf]], [SK[hf]])
                        if reset:
                            tk.op("dve", lambda e: e.tensor_scalar(S32[:, HS[hf], :], S32[:, HS[hf], :], self.flg[:, 0:1], None, ALU.mult), [SK[hf], "flg"], [SK[hf]])
                        tk.op("pool", lambda e: e.tensor_copy(out=Sb[:, HS[hf], :], in_=S32[:, HS[hf], :]), [SK[hf]], [SBK[hf]])
                    yield
                    last_in_tile = (bi == 3) if dr == 0 else (bi == 0)
                    if last_in_tile:
                        t0 = tt * 512
                        tk.dma(self.OD[dr][:, t0:t0 + 512].rearrange("(h p) t -> p h t", p=128), ost[oslot][:], f"{sid}o{oslot}", reads=[f"{sid}ost{oslot}"])

            gens = [stream(0), stream(1)]
            active = list(gens)
            next(gens[0]); next(gens[1])
            for _ in range(17):
                next(gens[0])
            while active:
                for g in list(active):
                    try:
                        next(g)
                    except StopIteration:
                        active.remove(g)
            tk.barrier()

    def pass3(self, l):
        self.uid += 1
        nc, tk, T = self.nc, self.tk, self.T
        NSB = T // 1024
        NT = self.NT
        with ExitStack() as st:
            sb = lambda n, s, dt: st.enter_context(nc.sbuf_tensor(self.nm(n), list(s), dt))
            ps = lambda n, s, dt: st.enter_context(nc.psum_tensor(self.nm(n), list(s), dt))
            qr = [sb(f"p3q{g}", [128, T], BF16) for g in range(3)]
            kr = [sb(f"p3k{g}", [128, T], BF16) for g in range(3)]
            vt = [sb(f"p3v{g}", [128, T // 128, 130], BF16) for g in range(3)]
            cs = sb("p3cos", [128, 512], F32)
            sn = sb("p3sin", [128, 512], F32)
            sq = [sb(f"p3sq{i}", [128, 512], BF16) for i in range(2)]
            qn = [sb(f"p3qn{i}", [128, 512], BF16) for i in range(2)]
            t1 = [sb(f"p3t1{i}", [128, 512], F32) for i in range(2)]
            t2 = [sb(f"p3t2{i}", [128, 512], F32) for i in range(2)]
            lnr = sb("p3lnr", [12, 512], F32)
            rs12 = lnr
            wq = sb("p3wq", [128, 1], F32)
            wk_ = sb("p3wk", [128, 1], F32)
            wrow = sb("p3wrow", [1, 128], F32)
            mx = sb("p3mx", [1, 4], F32)
            negB = sb("p3negB", [128, 1], F32)
            segrow = sb("p3seg", [1, 64], F32)
            pT = [sb(f"p3pT{i}", [128, 512], BF16) for i in range(2)]
            zb = [sb(f"p3zb{i}", [64, 1024], BF16) for i in range(2)]
            rrow = sb("p3rrow", [65, 1024], F32)
            bcs = sb("p3bcs", [64, 1024], F32)
            og = sb("p3og", [64, 1024], BF16)
            PN = [ps(f"p3PN{i}", [65, 1024], F32) for i in range(2)]
            PSC = [ps(f"p3PS{i}", [128, 512], F32) for i in range(2)]
            PP = [ps(f"p3PP{i}", [128, 512], F32) for i in range(2)]
            identb = self.CB(CB_ID)
            rotT = self.CB(CB_ROT)
            oh12 = self.cb[:, CB_OH12 * 128:CB_OH12 * 128 + 72].rearrange("p (c m) -> p c m", m=12)
            amc = self.cb[:, CB_AMC * 128:CB_AMC * 128 + 704]
            amb = self.cb[:, CB_AMB * 128:CB_AMB * 128 + 512]
            amd = self.cb[:, CB_AMD * 128:CB_AMD * 128 + 512]
            tk.dma(wq[:], self.qnw_c[l], "p3s", writes=["wq"])
            tk.dma(wk_[:], self.knw_c[l], "p3s", writes=["wk"])
            tk.dma(wrow[:, 0:64], self.qnw_r[l], "p3s", writes=["wrow"])
            tk.dma(wrow[:, 64:128], self.knw_r[l], "p3s", writes=["wrow"])
            tk.op("dve", lambda e: e.tensor_scalar(wq[:], wq[:], 0.125, None, ALU.mult), ["wq"], ["wq"])
            tk.op("dve", lambda e: e.tensor_reduce(out=mx[:, 0:1], in_=wrow[:, 0:64], axis=AX.X, op=ALU.max, apply_absolute_value=True), ["wrow"], ["mx"])
            tk.op("dve", lambda e: e.tensor_reduce(out=mx[:, 1:2], in_=wrow[:, 64:128], axis=AX.X, op=ALU.max, apply_absolute_value=True), ["wrow"], ["mx"])
            tk.op("dve", lambda e: e.scalar_tensor_tensor(mx[:, 2:3], mx[:, 0:1], -8.0, mx[:, 1:2], ALU.mult, ALU.mult), ["mx"], ["mx"])
            tk.op("pe", lambda e: e.matmul(PP[0][:, 0:1], lhsT=self.cf[0:1, CF_ONES * 128:(CF_ONES + 1) * 128], rhs=mx[0:1, 2:3], start=True, stop=True), ["mx", "cf"], ["PP0"])
            tk.op("dve", lambda e: e.tensor_copy(out=negB[:, 0:1], in_=PP[0][:, 0:1]), ["PP0"], ["negB"])
            tk.op("dve", lambda e: e.tensor_copy(out=segrow[:], in_=self.flg[0:1, 1:2].broadcast_to([1, 64])), ["flg"], ["segrow"])
            for g in range(3):
                tk.op("pool", lambda e: e.memset(vt[g][:, :, 64:65], 1.0), [], [f"vt{g}"])
                tk.op("pool", lambda e: e.memset(vt[g][:, :, 129:130], 1.0), [], [f"vt{g}"])
            DIL = (1, 4, 16)
            sci = [0]
            pni = [0]
            QK = lambda g, t: f"qr{g}_{t}"
            KK = lambda g, t: f"kr{g}_{t}"

            def tkeys(fn, g, c0, n, d):
                return [fn(g, t) for t in range(c0 // 512, (c0 + (n - 1) * d) // 512 + 1)]

            for hp in range(4):
                for g in range(3):
                    for c0 in range(0, T, 2048):
                        c1 = min(T, c0 + 2048)
                        tl = list(range(c0 // 512, c1 // 512))
                        tk.dma(qr[g][:, c0:c1], self.QKB[g * 512 + hp * 128:g * 512 + hp * 128 + 128, c0:c1], "p3lq", writes=[QK(g, t) for t in tl])
                        tk.dma(kr[g][:, c0:c1], self.QKB[1536 + g * 512 + hp * 128:1536 + g * 512 + hp * 128 + 128, c0:c1], "p3lq", writes=[KK(g, t) for t in tl])
                    d = DIL[g]
                    nt = T // 128 // d
                    for ee in range(2):
                        c0v = g * 512 + hp * 128 + 64 * ee
                        src = self.VB[:, c0v:c0v + 64].rearrange("(j p dd) c -> dd p j c", dd=d, p=128)
                        for z in range(d):
                            for j0 in range(0, nt, 8):
                                j1 = min(nt, j0 + 8)
                                tk.dma(vt[g][:, z * nt + j0:z * nt + j1, 65 * ee:65 * ee + 64], src[z][:, j0:j1, :], "p3lv", writes=[f"vt{g}"])
                for tt in range(NT):
                    t0 = tt * 512
                    tc_ = slice(t0, t0 + 512)
                    tk.dma(cs[:], self.c_cos[:, tc_], "p3cs", writes=["cs"])
                    tk.dma(sn[:], self.c_sin[:, tc_], "p3cs", writes=["sn"])
                    streams = [(qr[g], QK(g, tt), wq) for g in range(3)] + [(kr[g], KK(g, tt), wk_) for g in range(3)]
                    for c, (buf, bk, w_) in enumerate(streams):
                        tk.op("act", lambda e: e.activation(out=sq[c % 2][:], in_=buf[:, tc_], func=AF.Square), [bk], [f"sq{c % 2}"])
                        tk.op("pe", lambda e: e.matmul(PN[0][0:12, 0:512], lhsT=oh12[:, c, :], rhs=sq[c % 2][:], start=(c == 0), stop=(c == 5)),
                              [f"sq{c % 2}", "cb"], ["PN0"])
                    tk.op("act", lambda e: e.activation(out=lnr[:], in_=PN[0][0:12, 0:512], func=AF.Ln, scale=1.0 / 64.0, bias=self.epsc[0:12, 0:1]), ["PN0"], ["lnr", "rs12"])
                    tk.op("act", lambda e: e.activation(out=rs12[:], in_=lnr[:], func=AF.Exp, scale=-0.5), ["lnr"], ["lnr", "rs12"])
                    for c, (buf, bk, w_) in enumerate(streams):
                        j = c % 2
                        pbc, pbck = (PP[0], "PP0") if j == 0 else (PSC[0], "PSC0")
                        prt, prtk = (PP[1], "PP1") if j == 0 else (PSC[1], "PSC1")
                        tk.op("pe", lambda e: e.matmul(pbc[:], lhsT=self.csel12[:, c * 128:(c + 1) * 128], rhs=rs12[:], start=True, stop=True), ["rs12", "csel12"], [pbck])
                        tk.op("dve", lambda e: e.scalar_tensor_tensor(qn[j][:], buf[:, tc_], w_[:, 0:1], pbc[:], ALU.mult, ALU.mult), [pbck, bk, "wq", "wk"], [f"qn{j}"])
                        tk.op("pe", lambda e: e.matmul(prt[:], lhsT=rotT, rhs=qn[j][:], start=True, stop=True), [f"qn{j}", "cb"], [prtk])
                        tk.op("pool", lambda e: e.tensor_tensor(out=t1[j][:], in0=qn[j][:], in1=cs[:], op=ALU.mult), [f"qn{j}", "cs"], [f"t1{j}"])
                        tk.op("dve", lambda e: e.tensor_tensor(out=t2[j][:], in0=prt[:], in1=sn[:], op=ALU.mult), [prtk, "sn"], [f"t2{j}"])
                        tk.op("pool", lambda e: e.tensor_tensor(out=buf[:, tc_], in0=t1[j][:], in1=t2[j][:], op=ALU.add), [f"t1{j}", f"t2{j}"], [bk])
                for sbk in range(NSB):
                    s0 = sbk * 1024
                    for e_ in range(2):
                        hd = 2 * hp + e_
                        pr = slice(64 * e_, 64 * e_ + 64)
                        vc = slice(65 * e_, 65 * e_ + 65)
                        zslot = (sbk * 2 + e_) % 2
                        pn_i = pni[0] % 2
                        pni[0] += 1
                        PNc = PN[pn_i]
                        PNK = f"PN{pn_i}"
                        tk.dma(zb[zslot][:], self.ZB[hd * 64:hd * 64 + 64, s0:s0 + 1024], f"p3z{zslot}", writes=[f"zb{zslot}"])
                        batch = []
                        pending = []
                        tk.op("dve", lambda e: e.memset(PNc[:], 0.0), [], [PNK])

                        def pattern(mvs):
                            n = len(mvs)
                            if all(mvs[i] == (mvs[0] + i) % 4 for i in range(n)):
                                return amc[:, mvs[0] * 64:(mvs[0] + n) * 64]
                            if all(mvs[i] == (2, 3)[i % 2] for i in range(n)):
                                return amb[:, 0:n * 64]
                            if all(mvs[i] == (0, 1)[i % 2] for i in range(n)):
                                return amd[:, 0:n * 64]
                            return None

                        def flush():
                            if not batch:
                                return
                            slot = sci[0] % 2
                            sci[0] += 1
                            nb_ = len(batch)
                            n = nb_ * 64
                            mrhs = pattern([b_[5] for b_ in batch])
                            assert mrhs is not None
                            tk.op("pe", lambda e: e.matmul(PSC[slot][:, 0:n], lhsT=identb, rhs=mrhs, start=True, stop=False), ["cb"], [f"PSC{slot}"], signal=False)
                            for i_, (seg, g, kc0, qc0, d, mv, vti, ocs) in enumerate(batch):
                                cols = slice(i_ * 64, i_ * 64 + 64)
                                kcols = slice(kc0, kc0 + 127 * d + 1, d)
                                qcols = slice(qc0, qc0 + 63 * d + 1, d)
                                last = (i_ == nb_ - 1)
                                if seg:
                                    tk.op("pe", lambda e: e.matmul(PSC[slot][:, cols], lhsT=self.cf[0:1, CF_ONES * 128:(CF_ONES + 1) * 128], rhs=segrow[0:1, :], start=False, stop=False),
                                          ["cf", "segrow"], [f"PSC{slot}"], signal=False)
                                tk.op("pe", lambda e: e.matmul(PSC[slot][:, cols], lhsT=kr[g][pr, kcols], rhs=qr[g][pr, qcols], start=False, stop=last),
                                      tkeys(KK, g, kc0, 128, d) + tkeys(QK, g, qc0, 64, d), [f"PSC{slot}"], signal=last)
                            tk.op("act", lambda e: e.activation(out=pT[slot][:, 0:n], in_=PSC[slot][:, 0:n], func=AF.Exp, bias=negB[:, 0:1]),
                                  [f"PSC{slot}", "negB"], [f"pT{slot}"])
                            prev = list(pending)
                            pending.clear()
                            pending.append((slot, list(batch)))
                            batch.clear()
                            for (pslot_, pb) in prev:
                                emit_pv(pslot_, pb)

                        def emit_pv(slot, pb):
                            nb_ = len(pb)
                            for i_, (seg, g, kc0, qc0, d, mv, vti, ocs) in enumerate(pb):
                                for oi, (oc0, n_o, pc0) in enumerate(ocs):
                                    ocols = slice(oc0, oc0 + (n_o - 1) * d + 1, d)
                                    lastpv = (i_ == nb_ - 1 and oi == len(ocs) - 1)
                                    tk.op("pe", lambda e: e.matmul(PNc[:, ocols], lhsT=vt[g][:, vti, vc], rhs=pT[slot][:, i_ * 64 + pc0:i_ * 64 + pc0 + n_o],
                                                                   start=False, stop=False, skip_group_check=True),
                                          [f"vt{g}", f"pT{slot}"], [PNK], signal=lastpv)

                        for g in range(3):
                            d = DIL[g]
                            L = T // d
                            nt = L // 128
                            lb = (T // 2) // d
                            nun = 1024 // d // 64
                            u0 = (s0 // d) // 64
                            for z in range(d):
                                for u in range(u0, u0 + nun):
                                    if u % 2 == 1:
                                        cand = [((u - 1) // 2, 0), ((u + 1) // 2, 1)]
                                    else:
                                        cand = [(u // 2 - 1, 2), (u // 2, 3)]
                                    for (j, mv) in cand:
                                        if j < 0 or j >= nt:
                                            continue
                                        seg = 1 if ((128 * j >= lb) != (64 * u >= lb)) else 0
                                        qc0 = z + d * 64 * u
                                        kc0 = z + d * 128 * j
                                        oc0 = qc0 - s0
                                        if d == 16:
                                            ocs = [(oc0, 32, 0), (oc0 + 512, 32, 32)]
                                        else:
                                            ocs = [(oc0, 64, 0)]
                                        if batch and (len(batch) == 8 or pattern([b_[5] for b_ in batch] + [mv]) is None):
                                            flush()
                                        batch.append((seg, g, kc0, qc0, d, mv, z * nt + j, ocs))
                            flush()
                        for (pslot_, pb) in pending:
                            emit_pv(pslot_, pb)
                        pending.clear()
                        tk.op("act", lambda e: e.activation(out=rrow[64:65, :], in_=PNc[64:65, :], func=AF.Ln), [PNK], ["rrow"])
                        tk.op("act", lambda e: e.activation(out=rrow[64:65, :], in_=rrow[64:65, :], func=AF.Exp, scale=-1.0), ["rrow"], ["rrow"])
                        for hf in range(2):
                            tk.op("pe", lambda e: e.matmul(PP[hf][0:64, :], lhsT=self.cf[64:65, CF_ONES * 128:CF_ONES * 128 + 64], rhs=rrow[64:65, hf * 512:(hf + 1) * 512], start=True, stop=True),
                                  ["rrow", "cf"], [f"PP{hf}"])
                            tk.op("act", lambda e: e.activation(out=bcs[:, hf * 512:(hf + 1) * 512], in_=PP[hf][0:64, :], func=AF.Copy), [f"PP{hf}"], ["bcs"])
                        tk.op("dve", lambda e: e.tensor_tensor(out=bcs[:], in0=PNc[0:64, :], in1=bcs[:], op=ALU.mult), [PNK, "bcs"], ["bcs"])
                        tk.op("pool", lambda e: e.tensor_tensor(out=og[:], in0=bcs[:], in1=zb[zslot][:], op=ALU.mult), ["bcs", f"zb{zslot}"], ["og"])
                        tk.dma(self.OGB[hd * 64:hd * 64 + 64, s0:s0 + 1024], og[:], "p3o", reads=["og"])
            tk.barrier()

    def pass4(self, l, xin, xout):
        self.uid += 1
        nc, tk, T = self.nc, self.tk, self.T
        with ExitStack() as st:
            sb = lambda n, s, dt: st.enter_context(nc.sbuf_tensor(self.nm(n), list(s), dt))
            ps = lambda n, s, dt: st.enter_context(nc.psum_tensor(self.nm(n), list(s), dt))
            wa = sb("p4wa", [128, 8, 1024], BF16)
            wbt = sb("p4wb", [128, 4, 1024], BF16)
            wo = sb("p4wo", [128, 8, 1024], BF16)
            of = sb("p4of", [128, 8, 512], BF16)
            ob = sb("p4ob", [128, 8, 512], BF16)
            za = sb("p4za", [128, 8, 512], BF16)
            ogb = [sb(f"p4gb{i}", [128, 4, 512], BF16) for i in range(2)]
            mg = sb("p4mg", [128, 16, 512], BF16)
            xt = [sb(f"p4x{i}", [128, 1024], F32) for i in range(4)]
            xo = [sb(f"p4xo{i}", [128, 1024], F32) for i in range(2)]
            o32 = sb("p4o", [128, 8, 512], F32)
            sq = [sb(f"p4sq{i}", [128, 512], BF16) for i in range(2)]
            lnr = sb("p4lnr", [8, 512], F32)
            rs8 = sb("p4rs8", [8, 512], F32)
            tmp = [sb(f"p4tm{i}", [128, 512], F32) for i in range(2)]
            og = sb("p4og", [128, 8, 512], BF16)
            t1 = [sb(f"p4t1{i}", [128, 512], F32) for i in range(2)]
            t2 = [sb(f"p4t2{i}", [128, 512], F32) for i in range(2)]
            mT = sb("p4mT", [128, 8, 512], BF16)
            pms = [ps(f"p4ms{i}", [128, 512], F32) for i in range(2)]
            pya = [ps(f"p4ya{i}", [128, 512], F32) for i in range(2)]
            pyb = [ps(f"p4yb{i}", [128, 512], F32) for i in range(2)]
            pd = [ps(f"p4pd{i}", [128, 512], F32) for i in range(2)]
            tk.dma(wa[:], self.WA[l], "p4w", writes=["wa"])
            tk.dma(wbt[:], self.WB[l], "p4w", writes=["wb"])
            tk.dma(wo[:], self.WO[l], "p4w", writes=["wo"])
            oh8 = self.cb[:, CB_OH8 * 128:CB_OH8 * 128 + 64].rearrange("p (h m) -> p h m", m=8)

            def fm(dr, n, t0):
                return dr[0:n * 128, t0:t0 + 512].rearrange("(h p) t -> p h t", p=128)

            def loads_a(tt):
                t0 = tt * 512
                tk.dma(of[:], fm(self.OD[0], 8, t0), "p4a", writes=["of"])
                tk.dma(ob[:], fm(self.OD[1], 8, t0), "p4a", writes=["ob"])
                tk.dma(za[:], fm(self.ZA, 8, t0), "p4b", writes=["za"])

            def loads_b(tt):
                t0 = tt * 512
                i = tt % 2
                tk.dma(ogb[i][:], fm(self.OGB, 4, t0), f"p4g{i}", writes=[f"ogb{i}"])
                tk.dma(mg[:], fm(self.MG, 16, t0), "p4c", writes=["mg"])

            loads_a(0)
            loads_b(0)
            xi = [0]
            for tt in range(self.NT):
                t0 = tt * 512
                i = tt % 2
                for h in range(8):
                    j = h % 2
                    tk.op("pool", lambda e: e.tensor_tensor(out=o32[:, h, :], in0=of[:, h, :], in1=ob[:, h, :], op=ALU.add), ["of", "ob"], [f"o32{h}"])
                    tk.op("act", lambda e: e.activation(out=sq[j][:], in_=o32[:, h, :], func=AF.Square), [f"o32{h}"], [f"sq{j}"])
                    tk.op("pe", lambda e: e.matmul(pms[0][0:8, :], lhsT=oh8[:, h, :], rhs=sq[j][:], start=(h == 0), stop=(h == 7)), [f"sq{j}", "cb"], ["pms0"])
                tk.op("act", lambda e: e.activation(out=lnr[:], in_=pms[0][0:8, :], func=AF.Ln, scale=1.0 / 128.0, bias=self.epsc[0:8, 2:3]), ["pms0"], ["lnr"])
                tk.op("act", lambda e: e.activation(out=rs8[:], in_=lnr[:], func=AF.Exp, scale=-0.5), ["lnr"], ["rs8"])
                for h in range(8):
                    j = h % 2
                    tk.op("pe", lambda e: e.matmul(pms[1][:], lhsT=self.csel[:, h * 128:(h + 1) * 128], rhs=rs8[:], start=True, stop=True), ["rs8", "csel"], ["pms1"])
                    tk.op("dve", lambda e: e.tensor_tensor(out=tmp[j][:], in0=pms[1][:], in1=o32[:, h, :], op=ALU.mult), ["pms1", f"o32{h}"], [f"tmp{j}"])
                    tk.op("pool", lambda e: e.tensor_tensor(out=og[:, h, :], in0=tmp[j][:], in1=za[:, h, :], op=ALU.mult), [f"tmp{j}", "za"], [f"og{h}"])
                if tt + 1 < self.NT:
                    loads_a(tt + 1)
                xsl = []
                for s in range(4):
                    xs = xi[0] % 4
                    xi[0] += 1
                    xsl.append(xs)
                    tk.dma(xt[xs][:], xin[t0 + s * 128:t0 + (s + 1) * 128, :], f"p4x{xs}", writes=[f"x{xs}"])
                for oc in range(8):
                    j = oc % 2
                    for kc in range(8):
                        tk.op("pe", lambda e: e.matmul(pya[j][:], lhsT=wa[:, kc, oc * 128:(oc + 1) * 128], rhs=og[:, kc, :], start=(kc == 0), stop=(kc == 7)),
                              ["wa", f"og{kc}"], [f"pya{j}"], signal=(kc == 7))
                    for kc in range(4):
                        tk.op("pe", lambda e: e.matmul(pyb[j][:], lhsT=wbt[:, kc, oc * 128:(oc + 1) * 128], rhs=ogb[i][:, kc, :], start=(kc == 0), stop=(kc == 3)),
                              ["wb", f"ogb{i}"], [f"pyb{j}"], signal=(kc == 3))
                    tk.op("dve", lambda e: e.scalar_tensor_tensor(t1[j][:], mg[:, oc, :], 1.0, pya[j][:], ALU.add, ALU.mult), [f"pya{j}", "mg"], [f"t1{j}"])
                    tk.op("dve", lambda e: e.scalar_tensor_tensor(t2[j][:], mg[:, 8 + oc, :], 1.0, pyb[j][:], ALU.add, ALU.mult), [f"pyb{j}", "mg"], [f"t2{j}"])
                    tk.op("pool", lambda e: e.tensor_tensor(out=mT[:, oc, :], in0=t1[j][:], in1=t2[j][:], op=ALU.add), [f"t1{j}", f"t2{j}"], [f"mT{oc}"])
                if tt + 1 < self.NT:
                    loads_b(tt + 1)
                for s in range(4):
                    xs = xsl[s]
                    xos = (tt * 4 + s) % 2
                    for half in range(2):
                        j = half
                        for kc in range(8):
                            tk.op("pe", lambda e: e.matmul(pd[j][:], lhsT=mT[:, kc, s * 128:(s + 1) * 128], rhs=wo[:, kc, half * 512:(half + 1) * 512],
                                                           start=(kc == 0), stop=(kc == 7)), ["wo", f"mT{kc}"], [f"pd{j}"], signal=(kc == 7))
                        tk.op("dve", lambda e: e.tensor_tensor(out=xo[xos][:, half * 512:(half + 1) * 512], in0=pd[j][:],
                                                               in1=xt[xs][:, half * 512:(half + 1) * 512], op=ALU.add), [f"pd{j}", f"x{xs}"], [f"xo{xos}"])
                    tk.dma(xout[t0 + s * 128:t0 + (s + 1) * 128, :], xo[xos][:], f"p4o{xos}", reads=[f"xo{xos}"])
            tk.barrier()


CF_ID, CF_ONES, CF_TRI0, CF_TRI1, CF_TRC0, CF_TRC1 = range(6)
NCF = 6
(CB_ID, CB_ONES, CB_ROT, CB_OH8, CB_OH12) = range(5)
CB_MB0 = 5
CB_MB1 = 13
CB_IDR = 21
CB_BD32 = 29
CB_OFF64 = 37
CB_OFF128 = 45
CB_AMC = 53
CB_AMB = 59
CB_AMD = 63
NCB = 67


def make_consts(T):
    j = np.arange(128)[:, None]
    i = np.arange(128)[None, :]
    cf = np.zeros((128, NCF, 128), np.float32)
    cf[:, CF_ID] = np.eye(128)
    cf[:, CF_ONES] = 1.0
    cf[:, CF_TRI0] = (j <= i)
    cf[:, CF_TRI1] = (j >= i)
    cf[:, CF_TRC0] = 1.0 - cf[:, CF_TRI0]
    cf[:, CF_TRC1] = 1.0 - cf[:, CF_TRI1]
    cb = np.zeros((128, NCB, 128), np.float32)
    cb[:, CB_ID] = np.eye(128)
    cb[:, CB_ONES] = 1.0
    R = np.zeros((128, 128), np.float32)
    for b in (0, 64):
        for d in range(32):
            R[b + d, b + d + 32] = -1.0
            R[b + d + 32, b + d] = 1.0
    cb[:, CB_ROT] = R.T
    oh8 = np.zeros((128, 8, 8), np.float32)
    for h in range(8):
        oh8[:, h, h] = 1.0
    cb[:, CB_OH8].reshape(128, 128)[:, :64] = oh8.reshape(128, 64)
    oh12 = np.zeros((128, 6, 12), np.float32)
    for c in range(6):
        oh12[:64, c, 2 * c] = 1.0
        oh12[64:, c, 2 * c + 1] = 1.0
    cb[:, CB_OH12].reshape(128, 128)[:, :72] = oh12.reshape(128, 72)
    for h in range(8):
        cb[:, CB_MB0 + h] = np.where(i > j, 0.0, NEG)
        cb[:, CB_MB1 + h] = np.where(i < j, 0.0, NEG)
        cb[:, CB_IDR + h] = np.eye(128)
        cb[:, CB_BD32 + h] = ((j // 32) == (i // 32))
        cb[:, CB_OFF64 + h] = ((j // 64) == (i // 64)) & ((j // 32) != (i // 32))
        cb[:, CB_OFF128 + h] = ((j // 64) != (i // 64))
    a = np.arange(128)[:, None]
    b = np.arange(64)[None, :]
    lo, hi = (a < 64), (a >= 64)
    am = np.zeros((4, 128, 64), np.float32)
    am[0] = np.where((lo & (b <= a)) | hi, 0.0, NEG)
    am[1] = np.where(lo & (a <= b), 0.0, NEG)
    am[2] = np.where(hi & (b <= (a - 64)), 0.0, NEG)
    am[3] = np.where(lo | (hi & ((a - 64) <= b)), 0.0, NEG)
    cb[:, CB_AMC:CB_AMC + 6].reshape(128, 768)[:, :704] = np.concatenate([am[i % 4] for i in range(11)], axis=1)
    cb[:, CB_AMB:CB_AMB + 4].reshape(128, 512)[:, :] = np.concatenate([am[(2, 3)[i % 2]] for i in range(8)], axis=1)
    cb[:, CB_AMD:CB_AMD + 4].reshape(128, 512)[:, :] = np.concatenate([am[(0, 1)[i % 2]] for i in range(8)], axis=1)
    sel = np.zeros((8, 8, 128), np.float32)
    for h in range(8):
        sel[h, h, :] = 1.0
    sel12 = np.zeros((12, 6, 128), np.float32)
    for c in range(6):
        sel12[2 * c, c, :64] = 1.0
        sel12[2 * c + 1, c, 64:] = 1.0
    half = 32
    inv = (10000.0 ** (-np.arange(half, dtype=np.float32) / half)).astype(np.float32)
    pos = np.arange(T, dtype=np.float32)
    ang = (pos[None, :] * inv[np.arange(128) % 32][:, None]).astype(np.float32)
    epsc = np.zeros((128, 4), np.float32)
    epsc[:, 0] = EPS
    epsc[:, 1] = 128.0 * 4.0 * EPS
    epsc[:, 2] = 4.0 * EPS
    epsc[:, 3] = 1.0
    return dict(c_f32=cf.reshape(128, NCF * 128), c_bf=cb.reshape(128, NCB * 128).astype(ml_dtypes.bfloat16),
                c_sel=sel.reshape(8, 1024), c_sel12=sel12.reshape(12, 768), c_eps=epsc,
                c_cos=np.cos(ang).astype(np.float32), c_sin=np.sin(ang).astype(np.float32))


def layout_params(p, depth):
    out = {}
    out["w_in"] = np.ascontiguousarray(p["w_in"][:depth], dtype=np.float32)
    out["norm_w8"] = np.ascontiguousarray(p["norm_w"][:depth].reshape(depth, 8, 128).transpose(0, 2, 1), dtype=np.float32)
    out["conv_l"] = np.ascontiguousarray(p["conv_a"][:depth].reshape(depth, 5, 24, 128).transpose(0, 3, 2, 1).reshape(depth, 128, 120), dtype=np.float32)
    out["alog_r"] = np.ascontiguousarray(np.broadcast_to(p["a_log"][:depth].reshape(depth, 1, 16), (depth, 128, 16)), dtype=np.float32)
    out["dtb_r"] = np.ascontiguousarray(np.broadcast_to(p["dt_bias"][:depth].reshape(depth, 1, 16), (depth, 128, 16)), dtype=np.float32)
    out["anw_c"] = np.ascontiguousarray(p["a_norm_w"][:depth].reshape(depth, 128, 1), dtype=np.float32)
    out["qnw_c"] = np.ascontiguousarray(np.concatenate([p["q_norm_w"][:depth]] * 2, axis=1).reshape(depth, 128, 1), dtype=np.float32)
    out["knw_c"] = np.ascontiguousarray(np.concatenate([p["k_norm_w"][:depth]] * 2, axis=1).reshape(depth, 128, 1), dtype=np.float32)
    out["qnw_r"] = np.ascontiguousarray(p["q_norm_w"][:depth].reshape(depth, 1, 64), dtype=np.float32)
    out["knw_r"] = np.ascontiguousarray(p["k_norm_w"][:depth].reshape(depth, 1, 64), dtype=np.float32)
    out["w_a_out"] = np.ascontiguousarray(p["w_a_out"][:depth], dtype=np.float32)
    out["w_b_out"] = np.ascontiguousarray(p["w_b_out"][:depth], dtype=np.float32)
    out["w_out"] = np.ascontiguousarray(p["w_out"][:depth], dtype=np.float32)
    return out


def flags_for(f):
    fl = np.zeros((128, 2), np.float32)
    fl[:, 0] = f
    fl[:, 1] = NEG * (1.0 - f)
    return fl


def kernel(x_prompt, x_sample, norm_w, w_in, conv_a, a_log, dt_bias, a_norm_w, q_norm_w, k_norm_w, w_a_out, w_b_out, w_out):
    T, depth = 8192, 4
    prm = dict(norm_w=np.asarray(norm_w), w_in=np.asarray(w_in), conv_a=np.asarray(conv_a), a_log=np.asarray(a_log), dt_bias=np.asarray(dt_bias),
               a_norm_w=np.asarray(a_norm_w), q_norm_w=np.asarray(q_norm_w), k_norm_w=np.asarray(k_norm_w), w_a_out=np.asarray(w_a_out),
               w_b_out=np.asarray(w_b_out), w_out=np.asarray(w_out))
    shared = layout_params(prm, depth)
    shared.update(make_consts(T))
    xp = np.asarray(x_prompt, dtype=np.float32)
    xs = np.asarray(x_sample, dtype=np.float32)
    in_maps = []
    for c in range(8):
        m = dict(shared)
        if c < 4:
            m["x"] = np.ascontiguousarray(xp[c])
            m["flags"] = flags_for(1.0)
        else:
            m["x"] = np.ascontiguousarray(xs[2 * (c - 4):2 * (c - 4) + 2].reshape(T, D))
            m["flags"] = flags_for(0.0)
        in_maps.append(m)
    nc = Prog(T, depth).build()
    res = run_bass_kernel_spmd(nc, in_maps, core_ids=list(range(8)))
    yp = np.stack([np.asarray(res.results[c]["y"]) for c in range(4)], axis=0).astype(np.float32)
    ys = np.concatenate([np.asarray(res.results[c]["y"]).reshape(2, 4096, D) for c in range(4, 8)], axis=0).astype(np.float32)
    return (yp, ys)
```

```python
import numpy as np
import ml_dtypes
from contextlib import ExitStack
import concourse.bass as bass
import concourse.mybir as mybir
from concourse.bass_utils import run_bass_kernel_spmd

F32 = mybir.dt.float32
BF16 = mybir.dt.bfloat16
AF = mybir.ActivationFunctionType
ALU = mybir.AluOpType
AX = mybir.AxisListType

D = 1024
IN_COLS = 11296
EPS = 1e-6
NEG = -30000.0
SEC_QKVA, SEC_ZA, SEC_GATE, SEC_QKB, SEC_VB, SEC_ZB, SEC_MG = range(7)
CHUNKS = []
for i in range(24): CHUNKS.append((i * 128, 128, SEC_QKVA, i))
for i in range(8): CHUNKS.append((3072 + i * 128, 128, SEC_ZA, i))
CHUNKS.append((4096, 32, SEC_GATE, 0))
for i in range(24): CHUNKS.append((4128 + i * 128, 128, SEC_QKB, i))
for i in range(12): CHUNKS.append((7200 + i * 128, 128, SEC_VB, i))
for i in range(4): CHUNKS.append((8736 + i * 128, 128, SEC_ZB, i))
for i in range(16): CHUNKS.append((9248 + i * 128, 128, SEC_MG, i))
NCH = len(CHUNKS)
WGROUPS = [list(range(0, 8)), list(range(8, 16)), list(range(16, 24)), list(range(24, 32)), [32],
           list(range(33, 41)), list(range(41, 49)), list(range(49, 57)), list(range(57, 61)), list(range(61, 65)),
           list(range(65, 69)), list(range(69, 73)), list(range(73, 81)), list(range(81, 89))]


_PSUM_KEYS = {"ptr", "pg", "PN", "PD", "PS", "PT", "PHa", "PHb"}
for _i in range(8):
    _PSUM_KEYS.update({f"pm{_i}", f"PP{_i}", f"PR{_i}", f"PT{_i}", f"PSC{_i}", f"pms{_i}", f"pya{_i}", f"pyb{_i}", f"pd{_i}"})


def is_psum_key(k):
    if len(k) > 2 and k[0] == "s" and k[1] in "01":
        k = k[2:]
    return k in _PSUM_KEYS


class TK:
    def __init__(self, nc, stack):
        self.nc = nc
        self.stack = stack
        self.E = {}
        for n, e in (("pe", nc.tensor), ("dve", nc.vector), ("act", nc.scalar), ("pool", nc.gpsimd), ("sp", nc.sync)):
            sem = stack.enter_context(nc.semaphore("s_" + n))
            self.E[n] = dict(e=e, sem=sem, cnt=0, known={}, pend=[])
        self.ds = {}
        self.bufs = {}
        self.ninstr = 0

    def dsem(self, name):
        if name not in self.ds:
            self.ds[name] = dict(sem=self.stack.enter_context(self.nc.semaphore("d_" + name)), cnt=0)
        return self.ds[name]

    def _wait(self, en, reads, writes):
        E = self.E[en]
        own = "s_" + en
        need = {}

        def add(tok, raw):
            sname, sem, val = tok
            if sname == own and en == "pe":
                return
            if sname.startswith("d_"):
                val = max(val, self.ds[sname[2:]]["cnt"])
            if need.get(sname, (None, 0))[1] < val:
                need[sname] = (sem, val)

        for k in reads:
            b = self.bufs.get(k)
            if b and b["w"]:
                add(b["w"], True)
            if b and is_psum_key(k):
                for sname, (sem, val) in b["r"].items():
                    if sname != own:
                        add((sname, sem, val), False)
        for k in writes:
            b = self.bufs.get(k)
            if b:
                if b["w"]:
                    add(b["w"], False)
                for sname, (sem, val) in b["r"].items():
                    add((sname, sem, val), False)
        for sname, (sem, val) in need.items():
            if E["known"].get(sname, 0) < val:
                E["e"].wait_ge(sem, val)
                E["known"][sname] = val
                self.ninstr += 1

    def _record(self, tok, reads, writes):
        sname, sem, val = tok
        for k in reads:
            b = self.bufs.setdefault(k, dict(w=None, r={}))
            b["r"][sname] = (sem, val)
        for k in writes:
            self.bufs[k] = dict(w=tok, r={})

    def op(self, en, fn, reads=(), writes=(), signal=True):
        E = self.E[en]
        self._wait(en, reads, writes)
        ins = fn(E["e"])
        self.ninstr += 1
        if signal:
            E["cnt"] += 1
            ins.then_inc(E["sem"], 1)
            tok = ("s_" + en, E["sem"], E["cnt"])
            for (r, w) in E["pend"]:
                self._record(tok, r, w)
            E["pend"] = []
            self._record(tok, reads, writes)
        else:
            E["pend"].append((tuple(reads), tuple(writes)))
            tok = ("s_" + en, E["sem"], E["cnt"] + 1)
            self._record(tok, reads, writes)

    def dma(self, out, in_, sem, reads=(), writes=(), q="sp"):
        E = self.E[q]
        self._wait(q, reads, writes)
        S = self.dsem(sem)
        ins = E["e"].dma_start(out=out, in_=in_)
        S["cnt"] += 16
        ins.then_inc(S["sem"], 16)
        self.ninstr += 1
        self._record(("d_" + sem, S["sem"], S["cnt"]), reads, writes)

    def barrier(self):
        for en, E in self.E.items():
            assert not E["pend"], en
        for en, E in self.E.items():
            for on, O in self.E.items():
                if on != en and O["cnt"] > E["known"].get("s_" + on, 0):
                    E["e"].wait_ge(O["sem"], O["cnt"])
                    E["known"]["s_" + on] = O["cnt"]
            for sn, S in self.ds.items():
                if S["cnt"] > E["known"].get("d_" + sn, 0):
                    E["e"].wait_ge(S["sem"], S["cnt"])
                    E["known"]["d_" + sn] = S["cnt"]
        self.bufs = {}


class Prog:
    def __init__(self, T, depth, dbg=()):
        self.T = T
        self.depth = depth
        self.dbg = set(dbg)
        self.NT = T // 512
        self.NB = T // 128
        self.uid = 0

    def nm(self, n):
        return f"{n}_{self.uid}"

    def dram(self, name, shape, dt, kind="Internal"):
        if name in self.dbg:
            kind = "ExternalOutput"
        return self.nc.dram_tensor(name, list(shape), dt, kind=kind).ap()

    def build(self):
        T, depth = self.T, self.depth
        nc = bass.Bass("TRN2", target_bir_lowering=False)
        self.nc = nc
        I = lambda n, s, dt=F32: nc.dram_tensor(n, list(s), dt, kind="ExternalInput").ap()
        self.x_in = I("x", [T, D])
        self.w_in = I("w_in", [depth, D, IN_COLS])
        self.norm_w = I("norm_w8", [depth, 128, 8])
        self.conv = I("conv_l", [depth, 128, 24 * 5])
        self.alog = I("alog_r", [depth, 128, 16])
        self.dtb = I("dtb_r", [depth, 128, 16])
        self.anw = I("anw_c", [depth, 128, 1])
        self.qnw_c = I("qnw_c", [depth, 128, 1])
        self.knw_c = I("knw_c", [depth, 128, 1])
        self.qnw_r = I("qnw_r", [depth, 1, 64])
        self.knw_r = I("knw_r", [depth, 1, 64])
        self.w_a = I("w_a_out", [depth, 1024, 1024])
        self.w_b = I("w_b_out", [depth, 512, 1024])
        self.w_o = I("w_out", [depth, 1024, 1024])
        self.c_f32 = I("c_f32", [128, NCF * 128])
        self.c_bf = I("c_bf", [128, NCB * 128], BF16)
        self.c_sel = I("c_sel", [8, 1024])
        self.c_sel12 = I("c_sel12", [12, 768])
        self.c_eps = I("c_eps", [128, 4])
        self.c_cos = I("c_cos", [128, T])
        self.c_sin = I("c_sin", [128, T])
        self.fl = I("flags", [128, 2])
        self.y = nc.dram_tensor("y", [T, D], F32, kind="ExternalOutput").ap()
        self.W1 = self.dram("W1", [depth, NCH, 128, 1024], BF16)
        self.WA = self.dram("WA", [depth, 128, 8, 1024], BF16)
        self.WB = self.dram("WB", [depth, 128, 4, 1024], BF16)
        self.WO = self.dram("WO", [depth, 128, 8, 1024], BF16)
        self.QA = self.dram("QA", [3072, T], BF16)
        self.ZA = self.dram("ZA", [1024, T], BF16)
        self.GA = self.dram("GA", [T, 32], F32)
        self.QKB = self.dram("QKB", [3072, T], BF16)
        self.VB = self.dram("VB", [T, 1536], BF16)
        self.ZB = self.dram("ZB", [512, T], BF16)
        self.MG = self.dram("MG", [2048, T], BF16)
        self.OD = [self.dram("OF", [1024, T], BF16), self.dram("OB", [1024, T], BF16)]
        self.OGB = self.dram("OGB", [512, T], BF16)
        self.QF = [self.dram("QFq", [1024, T], BF16), self.dram("QFk", [1024, T], BF16)]
        self.TM = [self.dram("TMk", [T, 1024], BF16), self.dram("TMv", [T, 1024], BF16)]
        self.XS = [self.dram("XS0", [T, D], F32), self.dram("XS1", [T, D], F32)]

        with ExitStack() as gst:
            self.tk = TK(nc, gst)
            tk = self.tk
            sb = lambda n, s, dt: gst.enter_context(nc.sbuf_tensor(self.nm(n), list(s), dt))
            self.cf = sb("cf", [128, NCF * 128], F32)
            self.cb = sb("cb", [128, NCB * 128], BF16)
            self.csel = sb("csel", [8, 1024], F32)
            self.flg = sb("flg", [128, 2], F32)
            self.csel12 = sb("csel12", [12, 768], F32)
            self.epsc = sb("epsc", [128, 4], F32)
            self.onec = self.epsc[:, 3:4]
            tk.dma(self.csel12[:], self.c_sel12[:, :], "g0", writes=["csel12"])
            tk.dma(self.epsc[:], self.c_eps[:, :], "g0", writes=["epsc"])
            tk.dma(self.cf[:], self.c_f32[:, :], "g0", writes=["cf"])
            tk.dma(self.cb[:], self.c_bf[:, :], "g0", writes=["cb"])
            tk.dma(self.csel[:], self.c_sel[:, :], "g0", writes=["csel"])
            tk.dma(self.flg[:], self.fl[:, :], "g0", writes=["flg"])
            tk.barrier()
            for l in range(depth):
                self.pass0(l)
            tk.barrier()
            for l in range(depth):
                xin = self.x_in if l == 0 else self.XS[(l - 1) % 2]
                xout = self.y if l == depth - 1 else self.XS[l % 2]
                if "skip1" not in self.dbg:
                    self.pass1(l, xin)
                    tk.barrier()
                if "skip2" not in self.dbg:
                    self.pass2(l)
                    tk.barrier()
                if "skip3" not in self.dbg:
                    self.pass3(l)
                    tk.barrier()
                if "skip4" not in self.dbg:
                    self.pass4(l, xin, xout)
                    tk.barrier()
            tk.barrier()
        return nc

    def CF(self, i, n=1):
        return self.cf[:, i * 128:(i + n) * 128]

    def CB(self, i, n=1):
        return self.cb[:, i * 128:(i + n) * 128]

    def pass0(self, l):
        self.uid += 1
        nc, tk = self.nc, self.tk
        with ExitStack() as st:
            sb = lambda n, s, dt: st.enter_context(nc.sbuf_tensor(self.nm(n), list(s), dt))
            wf = [sb(f"p0f{i}", [128, 1024], F32) for i in range(3)]
            wb = [sb(f"p0b{i}", [128, 1024], BF16) for i in range(3)]
            nw = sb("p0nw", [128, 8], F32)
            an = sb("p0an", [128, 1], F32)
            tk.dma(nw[:], self.norm_w[l], "p0s", writes=["p0nw"])
            tk.dma(an[:], self.anw[l], "p0s", writes=["p0an"])
            half = sb("p0half", [128, 1], F32)
            tk.op("dve", lambda e: e.tensor_scalar(an[:], an[:], 0.5, None, ALU.mult), ["p0an"], ["p0an"])
            tk.op("pool", lambda e: e.memset(half[:], 0.5), [], ["p0an"])
            cnt = [0]
            engs = ["dve", "act", "dve"]

            wf4 = [sb(f"p0g{i}", [128, 1024], F32) for i in range(4)]
            stgw = [sb(f"p0s{i}", [128, 8, 8, 128], BF16) for i in range(2)]
            jcnt = 0
            for gi_, grp in enumerate(WGROUPS):
                c0 = CHUNKS[grp[0]][0]
                ncol = sum(CHUNKS[c][1] for c in grp)
                g = len(grp)
                sgi = gi_ % 2
                sg = stgw[sgi]
                sgk = f"p0sg{sgi}"
                if g == 1:
                    tk.op("pool", lambda e: e.memset(sg[:, 0, :, :], 0.0), [], [sgk])
                for kc in range(8):
                    i = jcnt % 4
                    jcnt += 1
                    tk.dma(wf4[i][:, 0:ncol], self.w_in[l, kc * 128:(kc + 1) * 128, c0:c0 + ncol], f"p0m{i}", writes=[f"p0g{i}"])
                    sc = nw[:, kc:kc + 1]
                    if g == 1:
                        dst_ap = sg[:, 0, kc, 0:ncol]
                        src_ap = wf4[i][:, 0:ncol]
                    else:
                        dst_ap = sg[:, 0:g, kc, :]
                        src_ap = wf4[i][:, 0:ncol].rearrange("p (g j) -> p g j", j=128)
                    if i % 2 == 0:
                        tk.op("act", lambda e: e.activation(out=dst_ap, in_=src_ap, func=AF.Copy, scale=sc), [f"p0g{i}", "p0nw"], [sgk])
                    else:
                        tk.op("dve", lambda e: e.tensor_scalar(dst_ap, src_ap, sc, None, ALU.mult), [f"p0g{i}", "p0nw"], [sgk])
                tk.dma(self.W1[l, grp[0]:grp[0] + g].rearrange("g p j -> p g j"), sg[:, 0:g].rearrange("p g k j -> p g (k j)"), f"p0u{sgi}", reads=[sgk])
            for (wsrc, wdst, nk, scale) in ((self.w_a, self.WA, 8, an), (self.w_b, self.WB, 4, half), (self.w_o, self.WO, 8, half)):
                for kc in range(nk):
                    i = cnt[0] % 3
                    cnt[0] += 1
                    tk.dma(wf[i][:, :], wsrc[l, kc * 128:(kc + 1) * 128, :], f"p0l{i}", writes=[f"p0f{i}"])
                    en = engs[i]
                    rd = [f"p0f{i}", "p0an"]
                    if scale is not None:
                        sc = scale[:, 0:1]
                        if en == "act":
                            tk.op(en, lambda e: e.activation(out=wb[i][:, :], in_=wf[i][:, :], func=AF.Copy, scale=sc), rd, [f"p0b{i}"])
                        else:
                            tk.op(en, lambda e: e.tensor_scalar(wb[i][:, :], wf[i][:, :], sc, None, ALU.mult), rd, [f"p0b{i}"])
                    else:
                        if en == "act":
                            tk.op(en, lambda e: e.activation(out=wb[i][:, :], in_=wf[i][:, :], func=AF.Copy), rd, [f"p0b{i}"])
                        else:
                            tk.op(en, lambda e: e.tensor_copy(out=wb[i][:, :], in_=wf[i][:, :]), rd, [f"p0b{i}"])
                    tk.dma(wdst[l, :, kc, :], wb[i][:, :], f"p0t{i}", reads=[f"p0b{i}"])
            tk.barrier()

    def pass1(self, l, xin):
        self.uid += 1
        nc, tk, T = self.nc, self.tk, self.T
        NSUP = min(4, self.NT)
        NST = self.NT // NSUP
        with ExitStack() as st:
            sb = lambda n, s, dt: st.enter_context(nc.sbuf_tensor(self.nm(n), list(s), dt))
            ps = lambda n, s, dt: st.enter_context(nc.psum_tensor(self.nm(n), list(s), dt))
            xt = [sb(f"p1x{i}", [128, 4, 1024], F32) for i in range(2)]
            hb = [sb(f"p1h{i}", [128, 1024], BF16) for i in range(2)]
            junk = sb("p1junk", [128, 1024], BF16)
            ss = [sb(f"p1ss{i}", [128, 4], F32) for i in range(2)]
            rs = [sb(f"p1rs{i}", [128, 4], F32) for i in range(2)]
            hT = [sb(f"p1hT{i}", [128, 8, 512 * NSUP], BF16) for i in range(2)]
            wsb = [sb(f"p1w{i}", [128, 8, 1024], BF16) for i in range(3)]
            stg = [sb(f"p1st{i}", [128, 4, 512], BF16) for i in range(4)]
            thb = [sb(f"p1th{i}", [128, 512], F32) for i in range(2)]
            vst = [sb(f"p1vs{i}", [128, 512], BF16) for i in range(3)]
            gpre = sb("p1gp", [128, 4, 32], F32)
            gsp = sb("p1gs", [128, 4, 32], F32)
            gout = [sb(f"p1go{i}", [128, 4, 32], F32) for i in range(2)]
            nea = sb("p1nea", [128, 16], F32)
            dtb = sb("p1dtb", [128, 16], F32)
            ptr = ps("p1ptr", [128, 8, 128], BF16)
            pmm = [ps(f"p1pm{i}", [128, 512], F32) for i in range(6)]
            pg = ps("p1pg", [128, 4, 32], F32)
            NPM = 6
            tk.dma(nea[:], self.alog[l], "p1s", writes=["nea"])
            tk.dma(dtb[:], self.dtb[l], "p1s", writes=["dtb"])
            tk.op("act", lambda e: e.activation(out=nea[:], in_=nea[:], func=AF.Exp), ["nea"], ["nea"])
            tk.op("dve", lambda e: e.tensor_scalar(nea[:], nea[:], -1.0, None, ALU.mult), ["nea"], ["nea"])
            identb = self.CB(CB_ID)
            pmi, sti, vsi, thi, xcnt, gcnt = [0], [0], [0], [0], [0], [0]
            groups = [WGROUPS[4]] + WGROUPS[:4] + WGROUPS[5:]
            NG = len(groups)

            def load_x(sti_, sub):
                t0 = (sti_ * NSUP + sub) * 512
                xi = (sti_ * NSUP + sub) % 2
                tk.dma(xt[xi][:], xin[t0:t0 + 512, :].rearrange("(s p) d -> p s d", p=128), f"p1x{xi}", writes=[f"xt{xi}"])

            def norm_sub(sti_, sub):
                xi = (sti_ * NSUP + sub) % 2
                hs = sti_ % 2
                for s in range(4):
                    tk.op("act", lambda e: e.activation(out=junk[:], in_=xt[xi][:, s, :], func=AF.Square, scale=1.0 / 32.0,
                                                        accum_out=ss[xi][:, s:s + 1]), [f"xt{xi}"], ["junk", f"ss{xi}"])
                tk.op("act", lambda e: e.activation(out=rs[xi][:], in_=ss[xi][:], func=AF.Ln, bias=self.epsc[:, 0:1]), [f"ss{xi}"], [f"rs{xi}"])
                tk.op("act", lambda e: e.activation(out=rs[xi][:], in_=rs[xi][:], func=AF.Exp, scale=-0.5), [f"rs{xi}"], [f"rs{xi}"])
                for s in range(4):
                    h = hb[s % 2]
                    hk = f"hb{s % 2}"
                    tk.op("act", lambda e: e.activation(out=h[:], in_=xt[xi][:, s, :], func=AF.Copy, scale=rs[xi][:, s:s + 1]),
                          [f"xt{xi}", f"rs{xi}"], [hk])
                    for kc in range(8):
                        tk.op("pe", lambda e: e.transpose(ptr[:, kc, :], h[:, kc * 128:(kc + 1) * 128], identb), [hk, "cb"], ["ptr"], signal=(kc == 7))
                    c0 = sub * 512 + s * 128
                    tk.op("dve", lambda e: e.tensor_copy(out=hT[hs][:, :, c0:c0 + 128], in_=ptr[:]), ["ptr"], [f"hT{hs}_{sub}"])

            jobs = [(sti_, gi) for sti_ in range(NST) for gi in range(NG)]

            def load_w(ji):
                sti_, gi = jobs[ji]
                grp = groups[gi]
                wslot = ji % 3
                g = len(grp)
                tk.dma(wsb[wslot][:, 0:g, :], self.W1[l, grp[0]:grp[0] + g].rearrange("g p j -> p g j"), f"p1w{wslot}", writes=[f"w{wslot}"])

            load_x(0, 0)
            if NSUP > 1:
                load_x(0, 1)
            load_w(0)
            load_w(1)
            for sub in range(NSUP):
                norm_sub(0, sub)
                if sub + 2 < NSUP:
                    load_x(0, sub + 2)
            for ji, (sti_, gi) in enumerate(jobs):
                if ji + 2 < len(jobs):
                    load_w(ji + 2)
                if sti_ + 1 < NST:
                    if gi == 0:
                        load_x(sti_ + 1, 0)
                        if NSUP > 1:
                            load_x(sti_ + 1, 1)
                    if 1 <= gi <= NSUP:
                        sub = gi - 1
                        norm_sub(sti_ + 1, sub)
                        if sub + 2 < NSUP:
                            load_x(sti_ + 1, sub + 2)
                grp = groups[gi]
                g = len(grp)
                wslot = ji % 3
                w = wsb[wslot]
                wk = f"w{wslot}"
                hs = sti_ % 2
                hTt = hT[hs]
                kind = CHUNKS[grp[0]][2]
                for sub in range(NSUP):
                    t0 = (sti_ * NSUP + sub) * 512
                    hTk = f"hT{hs}_{sub}"
                    sc = slice(sub * 512, sub * 512 + 512)
                    if kind in (SEC_QKVA, SEC_ZA, SEC_QKB, SEC_ZB, SEC_MG):
                        dst = {SEC_QKVA: self.QA, SEC_ZA: self.ZA, SEC_QKB: self.QKB, SEC_ZB: self.ZB, SEC_MG: self.MG}[kind]
                        for q0 in range(0, g, 4):
                            sslot = sti[0] % 4
                            sti[0] += 1
                            sg = stg[sslot]
                            for gi_ in range(q0, q0 + 4):
                                p = pmm[pmi[0] % NPM]
                                pk = f"pm{pmi[0] % NPM}"
                                pmi[0] += 1
                                for kc in range(8):
                                    tk.op("pe", lambda e: e.matmul(p[:], lhsT=w[:, gi_, kc * 128:(kc + 1) * 128], rhs=hTt[:, kc, sc],
                                                                   start=(kc == 0), stop=(kc == 7)), [wk, hTk], [pk], signal=(kc == 7))
                                if kind in (SEC_QKVA, SEC_QKB):
                                    if pmi[0] % 2 == 0:
                                        tk.op("act", lambda e: e.activation(out=sg[:, gi_ - q0, :], in_=p[:], func=AF.Copy), [pk], [f"stg{sslot}"])
                                    else:
                                        tk.op("dve", lambda e: e.tensor_copy(out=sg[:, gi_ - q0, :], in_=p[:]), [pk], [f"stg{sslot}"])
                                elif kind == SEC_MG:
                                    tk.op("act", lambda e: e.activation(out=sg[:, gi_ - q0, :], in_=p[:], func=AF.Tanh, scale=0.5), [pk], [f"stg{sslot}"])
                                else:
                                    th = thb[thi[0] % 2]
                                    thk = f"th{thi[0] % 2}"
                                    thi[0] += 1
                                    tk.op("act", lambda e: e.activation(out=th[:], in_=p[:], func=AF.Tanh, scale=0.5), [pk], [thk])
                                    tk.op("dve", lambda e: e.scalar_tensor_tensor(sg[:, gi_ - q0, :], th[:], 1.0, p[:], ALU.add, ALU.mult), [pk, thk], [f"stg{sslot}"])
                            r0 = CHUNKS[grp[q0]][3] * 128
                            tk.dma(dst[r0:r0 + 512, t0:t0 + 512].rearrange("(g p) t -> p g t", p=128), sg[:], f"p1st{sslot}", reads=[f"stg{sslot}"])
                    elif kind == SEC_VB:
                        vq = CHUNKS[grp[0]][3] // 4
                        for s in range(4):
                            p = pmm[pmi[0] % NPM]
                            pk = f"pm{pmi[0] % NPM}"
                            pmi[0] += 1
                            for kc in range(8):
                                tk.op("pe", lambda e: e.matmul(p[:], lhsT=hTt[:, kc, sub * 512 + s * 128:sub * 512 + (s + 1) * 128], rhs=w[:, 0:4, kc * 128:(kc + 1) * 128],
                                                               start=(kc == 0), stop=(kc == 7)), [wk, hTk], [pk], signal=(kc == 7))
                            vslot = vsi[0] % 3
                            vsi[0] += 1
                            if vsi[0] % 2 == 0:
                                tk.op("act", lambda e: e.activation(out=vst[vslot][:], in_=p[:], func=AF.Copy), [pk], [f"vst{vslot}"])
                            else:
                                tk.op("dve", lambda e: e.tensor_copy(out=vst[vslot][:], in_=p[:]), [pk], [f"vst{vslot}"])
                            tk.dma(self.VB[t0 + s * 128:t0 + (s + 1) * 128, vq * 512:(vq + 1) * 512], vst[vslot][:], f"p1vs{vslot}", reads=[f"vst{vslot}"])
                    else:
                        gs_ = gcnt[0] % 2
                        gcnt[0] += 1
                        go = gout[gs_]
                        gk = f"go{gs_}"
                        for s in range(4):
                            for kc in range(8):
                                tk.op("pe", lambda e: e.matmul(pg[:, s, :], lhsT=hTt[:, kc, sub * 512 + s * 128:sub * 512 + (s + 1) * 128], rhs=w[:, 0, kc * 128:kc * 128 + 32],
                                                               start=(kc == 0), stop=(kc == 7)), [wk, hTk], ["pg"], signal=(kc == 7 and s == 3))
                        dtb_b = dtb[:, :].unsqueeze(1).broadcast_to([128, 4, 16])
                        nea_b = nea[:, :].unsqueeze(1).broadcast_to([128, 4, 16])
                        tk.op("dve", lambda e: e.tensor_tensor(out=gpre[:, :, 0:16], in0=pg[:, :, 0:16], in1=dtb_b, op=ALU.add), ["pg", "dtb"], ["gpre"])
                        tk.op("dve", lambda e: e.tensor_scalar(gpre[:, :, 16:32], pg[:, :, 16:32], -1.0, None, ALU.mult), ["pg"], ["gpre"])
                        tk.op("act", lambda e: e.activation(out=gsp[:], in_=gpre[:], func=AF.Exp), ["gpre"], ["gsp"])
                        tk.op("act", lambda e: e.activation(out=gsp[:], in_=gsp[:], func=AF.Ln, bias=self.onec[:, 0:1]), ["gsp"], ["gsp"])
                        tk.op("dve", lambda e: e.tensor_tensor(out=go[:, :, 0:16], in0=gsp[:, :, 0:16], in1=nea_b, op=ALU.mult), ["gsp", "nea"], [gk])
                        tk.op("dve", lambda e: e.tensor_scalar(go[:, :, 16:32], gsp[:, :, 16:32], -1.0, None, ALU.mult), ["gsp"], [gk])
                        tk.dma(self.GA[t0:t0 + 512, :].rearrange("(s p) c -> p s c", p=128), go[:], f"p1go{gs_}", reads=[gk])
            tk.barrier()

    def pass2(self, l):
        self.pass2a(l)
        self.tk.barrier()
        if "only2a" not in self.dbg:
            self.pass2b(l)

    def pass2a(self, l):
        self.uid += 1
        nc, tk, T = self.nc, self.tk, self.T
        with ExitStack() as st:
            sb = lambda n, s, dt: st.enter_context(nc.sbuf_tensor(self.nm(n), list(s), dt))
            ps = lambda n, s, dt: st.enter_context(nc.psum_tensor(self.nm(n), list(s), dt))
            cw = sb("a2cw", [128, 120], F32)
            dg = sb("a2dg", [128, 120, 128], BF16)
            raw = [sb(f"a2raw{i}", [128, 8, 516], BF16) for i in range(3)]
            out = [sb(f"a2out{i}", [128, 8, 512], BF16) for i in range(3)]
            th = [sb(f"a2th{i}", [128, 512], F32) for i in range(3)]
            sq = [sb(f"a2sq{i}", [128, 512], BF16) for i in range(2)]
            lnr = sb("a2lnr", [8, 2, 512], F32)
            rs8 = sb("a2rs8", [8, 2, 512], F32)
            tm = [sb(f"a2tm{i}", [128, 8, 128], BF16) for i in range(4)]
            PP = [ps(f"a2PP{i}", [128, 512], F32) for i in range(3)]
            PR = [ps(f"a2PR{i}", [128, 512], F32) for i in range(2)]
            PT = [ps(f"a2PT{i}", [128, 8, 128], BF16) for i in range(2)]
            identb = self.CB(CB_ID)
            oh8 = self.cb[:, CB_OH8 * 128:CB_OH8 * 128 + 64].rearrange("p (h m) -> p h m", m=8)
            tk.dma(cw[:], self.conv[l], "a2s", writes=["cw"])
            for c in range(120):
                tk.op("pool", lambda e: e.tensor_scalar(dg[:, c, :], identb, cw[:, c:c + 1], None, ALU.mult), ["cw", "cb"], ["dg"])
            jobs = [(tt, which) for tt in range(self.NT) for which in range(3)]

            def load_raw(ji):
                tt, which = jobs[ji]
                t0 = tt * 512
                r = raw[ji % 3]
                rk = f"raw{ji % 3}"
                lo, hi = max(t0 - 2, 0), min(t0 + 514, T)
                if t0 == 0:
                    tk.op("pool", lambda e: e.memset(r[:, :, 0:2], 0.0), [], [rk])
                if t0 + 512 == T:
                    tk.op("pool", lambda e: e.memset(r[:, :, 514:516], 0.0), [], [rk])
                tk.dma(r[:, :, lo - (t0 - 2):hi - (t0 - 2)],
                       self.QA[which * 1024:(which + 1) * 1024, lo:hi].rearrange("(h p) t -> p h t", p=128), f"a2raw{ji % 3}", writes=[rk])
                if t0 + 512 == T // 2:
                    tk.op("pool", lambda e: e.tensor_scalar(r[:, :, 514:516], r[:, :, 514:516], self.flg[:, 0:1], None, ALU.mult), [rk, "flg"], [rk])
                if t0 == T // 2:
                    tk.op("pool", lambda e: e.tensor_scalar(r[:, :, 0:2], r[:, :, 0:2], self.flg[:, 0:1], None, ALU.mult), [rk, "flg"], [rk])

            load_raw(0)
            load_raw(1)
            ppi, tmi, pti = [0], [0], [0]
            for tt in range(self.NT):
                t0 = tt * 512
                for which in range(3):
                    ji = tt * 3 + which
                    if ji + 2 < len(jobs):
                        load_raw(ji + 2)
                    r = raw[ji % 3]
                    rk = f"raw{ji % 3}"
                    dst = out[which]
                    dk_ = f"out{which}"
                    for h in range(8):
                        c = which * 8 + h
                        pj = ppi[0] % 3
                        ppi[0] += 1
                        pp, ppk = PP[pj], f"PP{pj}"
                        for j in range(5):
                            tk.op("pe", lambda e: e.matmul(pp[:], lhsT=dg[:, c * 5 + j, :], rhs=r[:, h, j:j + 512], start=(j == 0), stop=(j == 4)),
                                  ["dg", rk], [ppk], signal=(j == 4))
                        tk.op("act", lambda e: e.activation(out=th[pj][:], in_=pp[:], func=AF.Tanh, scale=0.5), [ppk], [f"th{pj}"])
                        tk.op("dve", lambda e: e.scalar_tensor_tensor(dst[:, h, :], th[pj][:], 1.0, pp[:], ALU.add, ALU.mult), [ppk, f"th{pj}"], [dk_])
                for which in range(2):
                    dst, dk_ = out[which], f"out{which}"
                    for h in range(8):
                        tk.op("act", lambda e: e.activation(out=sq[h % 2][:], in_=dst[:, h, :], func=AF.Square), [dk_], [f"sq{h % 2}"])
                        tk.op("pe", lambda e: e.matmul(PR[which][0:8, :], lhsT=oh8[:, h, :], rhs=sq[h % 2][:], start=(h == 0), stop=(h == 7)),
                              [f"sq{h % 2}", "cb"], [f"PR{which}"])
                tk.op("act", lambda e: e.activation(out=lnr[:, 0, :], in_=PR[0][0:8, :], func=AF.Ln, scale=128.0, bias=self.epsc[0:8, 1:2]), ["PR0"], ["lnr"])
                tk.op("act", lambda e: e.activation(out=lnr[:, 1, :], in_=PR[1][0:8, :], func=AF.Ln, bias=self.epsc[0:8, 2:3]), ["PR1"], ["lnr"])
                tk.op("act", lambda e: e.activation(out=rs8[:], in_=lnr[:], func=AF.Exp, scale=-0.5), ["lnr"], ["rs8"])
                for which in range(2):
                    dst, dk_ = out[which], f"out{which}"
                    for h in range(8):
                        pj = ppi[0] % 3
                        ppi[0] += 1
                        pp, ppk = PP[pj], f"PP{pj}"
                        tk.op("pe", lambda e: e.matmul(pp[:], lhsT=self.csel[:, h * 128:(h + 1) * 128], rhs=rs8[:, which, :], start=True, stop=True), ["rs8", "csel"], [ppk])
                        tk.op("dve", lambda e: e.tensor_tensor(out=dst[:, h, :], in0=dst[:, h, :], in1=pp[:], op=ALU.mult), [ppk, dk_], [dk_])
                    tk.dma(self.QF[which][:, t0:t0 + 512].rearrange("(h p) t -> p h t", p=128), dst[:], f"a2o{which}", reads=[dk_])
                for which in (1, 2):
                    dst, dk_ = out[which], f"out{which}"
                    for bi in range(4):
                        pt_i = pti[0] % 2
                        pti[0] += 1
                        for h in range(8):
                            tk.op("pe", lambda e: e.transpose(PT[pt_i][:, h, :], dst[:, h, bi * 128:(bi + 1) * 128], identb), [dk_, "cb"], [f"PT{pt_i}"], signal=(h == 7))
                        ti_ = tmi[0] % 4
                        tmi[0] += 1
                        if ti_ % 2 == 0:
                            tk.op("act", lambda e: e.activation(out=tm[ti_][:], in_=PT[pt_i][:], func=AF.Copy), [f"PT{pt_i}"], [f"tm{ti_}"])
                        else:
                            tk.op("dve", lambda e: e.tensor_copy(out=tm[ti_][:], in_=PT[pt_i][:]), [f"PT{pt_i}"], [f"tm{ti_}"])
                        tk.dma(self.TM[which - 1][t0 + bi * 128:t0 + (bi + 1) * 128, :], tm[ti_][:].rearrange("p h d -> p (h d)"), f"a2t{ti_}", reads=[f"tm{ti_}"])
            tk.barrier()

    def pass2b(self, l):
        self.uid += 1
        nc, tk, T = self.nc, self.tk, self.T
        with ExitStack() as st:
            sb = lambda n, s, dt: st.enter_context(nc.sbuf_tensor(self.nm(n), list(s), dt))
            ps = lambda n, s, dt: st.enter_context(nc.psum_tensor(self.nm(n), list(s), dt))
            identb, identf, onesf = self.CB(CB_ID), self.CF(CF_ID), self.CF(CF_ONES)
            c3 = lambda blk: self.cb[:, blk * 128:(blk + 8) * 128].rearrange("p (h i) -> p h i", i=128)
            idr, bd32, off64, off128 = c3(CB_IDR), c3(CB_BD32), c3(CB_OFF64), c3(CB_OFF128)
            SLOTS = ["qg", "PTn", "Vb", "Ao64", "Ao128", "Ya", "Yb", "Ma", "Mb", "Aa", "Ab", "X", "Z", "Rn", "nvb", "nvd"]
            ALIAS = {"T1": "Z", "Ds": "X", "Dq": "Rn", "M0": "nvb", "A": "nvd", "A32x": "Ab", "M": "Mb", "M32": "Mb"}
            NB = self.NB
            HS = [slice(0, 4), slice(4, 8)]

            def stream(dr):
                sid = f"s{dr}"
                K = lambda n, hf: sid + ALIAS.get(n, n) + "ab"[hf]
                Hs = {n: sb(f"b2{sid}{n}", [128, 8, 128], BF16) for n in SLOTS}
                H = lambda n, hf: Hs[ALIAS.get(n, n)][:, HS[hf], :]
                H1 = lambda n, h: Hs[ALIAS.get(n, n)][:, h, :]
                qf = [sb(f"b2{sid}qf{i}", [128, 8, 128], BF16) for i in range(2)]
                kf = [sb(f"b2{sid}kf{i}", [128, 8, 128], BF16) for i in range(2)]
                ktm = [sb(f"b2{sid}kt{i}", [128, 8, 128], BF16) for i in range(2)]
                vtm = [sb(f"b2{sid}vt{i}", [128, 8, 128], BF16) for i in range(2)]
                ga = [sb(f"b2{sid}ga{i}", [128, 32], F32) for i in range(2)]
                rhsD = sb(f"b2{sid}rhsD", [128, 8, 128], F32)
                ngc = sb(f"b2{sid}ngc", [8, 128], F32)
                ex = sb(f"b2{sid}ex", [128, 5, 8], F32)
                S32 = sb(f"b2{sid}S32", [128, 8, 128], F32)
                Sb = sb(f"b2{sid}Sb", [128, 8, 128], BF16)
                tq = [sb(f"b2{sid}tq{i}", [128, 4, 128], F32) for i in range(2)]
                tq2 = [sb(f"b2{sid}tq2{i}", [128, 4, 128], F32) for i in range(2)]
                ost = [sb(f"b2{sid}ost{i}", [128, 8, 512], BF16) for i in range(2)]
                PH = ps(f"b2{sid}PH", [128, 8, 128], F32)
                PT = ps(f"b2{sid}PT", [128, 8, 128], BF16)
                PS = ps(f"b2{sid}PS", [128, 512], F32)
                PHA = [sid + "PHa", sid + "PHb"]
                PTK = [sid + "PT"]
                PSK = sid + "PS"
                P0 = PH[:].rearrange("p h i -> p (h i)")
                tri = self.CF(CF_TRI0 + dr)
                trc = self.CF(CF_TRC0 + dr)
                mb = self.cb[:, (CB_MB0 + 8 * dr) * 128:(CB_MB0 + 8 * dr + 8) * 128]
                order = list(range(NB)) if dr == 0 else list(range(NB - 1, -1, -1))

                def load_blk(n):
                    gb = order[n]
                    i = n % 2
                    b0 = gb * 128
                    tk.dma(qf[i][:], self.QF[0][:, b0:b0 + 128].rearrange("(h p) t -> p h t", p=128), f"{sid}lq{i}", writes=[f"{sid}qf{i}"])
                    tk.dma(kf[i][:], self.QF[1][:, b0:b0 + 128].rearrange("(h p) t -> p h t", p=128), f"{sid}lq{i}", writes=[f"{sid}kf{i}"])
                    tk.dma(ktm[i][:].rearrange("p h d -> p (h d)"), self.TM[0][b0:b0 + 128, :], f"{sid}lt{i}", writes=[f"{sid}kt{i}"])
                    tk.dma(vtm[i][:].rearrange("p h d -> p (h d)"), self.TM[1][b0:b0 + 128, :], f"{sid}lt{i}", writes=[f"{sid}vt{i}"])
                    tk.dma(ga[i][:], self.GA[b0:b0 + 128, :], f"{sid}lg{i}", writes=[f"{sid}ga{i}"])

                def hmm(Ln, Rn_, lhs_fn=None, rhs_fn=None, extra_reads=(), add=None):
                    for hf in range(2):
                        rd = list(extra_reads)
                        if Ln:
                            rd.append(K(Ln, hf))
                        if Rn_:
                            rd.append(K(Rn_, hf))
                        if add:
                            rd += [K(add, hf), "cb"]
                        for h in range(4 * hf, 4 * hf + 4):
                            lh = lhs_fn(h) if lhs_fn else H1(Ln, h)
                            rh = rhs_fn(h) if rhs_fn else H1(Rn_, h)
                            if add:
                                tk.op("pe", lambda e: e.matmul(PH[:, h, :], lhsT=lh, rhs=rh, start=True, stop=False), rd, [PHA[hf]], signal=False)
                                tk.op("pe", lambda e: e.matmul(PH[:, h, :], lhsT=identb, rhs=H1(add, h), start=False, stop=True), rd, [PHA[hf]], signal=(h % 4 == 3))
                            else:
                                tk.op("pe", lambda e: e.matmul(PH[:, h, :], lhsT=lh, rhs=rh, start=True, stop=True), rd, [PHA[hf]], signal=(h % 4 == 3))

                def htr(src, reads_fn):
                    for hf in range(2):
                        for h in range(4 * hf, 4 * hf + 4):
                            tk.op("pe", lambda e: e.transpose(PT[:, h, :], src(h), identb), list(reads_fn(hf)) + ["cb"], PTK, signal=(h % 4 == 3))

                def cp(dst_, neg=False):
                    for hf in range(2):
                        if True:
                            tk.op("act", lambda e: e.activation(out=H(dst_, hf), in_=PH[:, HS[hf], :], func=AF.Copy, scale=(-1.0 if neg else 1.0)), [PHA[hf]], [K(dst_, hf)])
                        elif neg:
                            tk.op("dve", lambda e: e.tensor_scalar(H(dst_, hf), PH[:, HS[hf], :], -1.0, None, ALU.mult), [PHA[hf]], [K(dst_, hf)])
                        else:
                            tk.op("dve", lambda e: e.tensor_copy(out=H(dst_, hf), in_=PH[:, HS[hf], :]), [PHA[hf]], [K(dst_, hf)])

                def cpT(dst_, eng="act", mul=None):
                    for hf in range(2):
                        if mul is None:
                            tk.op("act", lambda e: e.activation(out=H(dst_, hf), in_=PT[:, HS[hf], :], func=AF.Copy), PTK, [K(dst_, hf)])
                        elif mul is None:
                            tk.op("dve", lambda e: e.tensor_copy(out=H(dst_, hf), in_=PT[:, HS[hf], :]), PTK, [K(dst_, hf)])
                        else:
                            tk.op("dve", lambda e: e.tensor_tensor(out=H(dst_, hf), in0=PT[:, HS[hf], :], in1=mul(hf), op=ALU.mult), PTK + [sid + "ex"], [K(dst_, hf)])

                def yadd(ysrc, ydst, sign=1.0):
                    for hf in range(2):
                        if sign > 0:
                            tk.op("dve", lambda e: e.tensor_tensor(out=H(ydst, hf), in0=PH[:, HS[hf], :], in1=H(ysrc, hf), op=ALU.add), [PHA[hf], K(ysrc, hf)], [K(ydst, hf)])
                        else:
                            tk.op("dve", lambda e: e.scalar_tensor_tensor(H(ydst, hf), PH[:, HS[hf], :], -1.0, H(ysrc, hf), ALU.mult, ALU.add), [PHA[hf], K(ysrc, hf)], [K(ydst, hf)])

                def pool2(dst_, a_, b_c, op, a_first=True):
                    for hf in range(2):
                        cs_ = b_c[:, HS[hf], :]
                        if a_first:
                            tk.op("pool", lambda e: e.tensor_tensor(out=H(dst_, hf), in0=H(a_, hf), in1=cs_, op=op), [K(a_, hf), "cb"], [K(dst_, hf)])
                        else:
                            tk.op("pool", lambda e: e.tensor_tensor(out=H(dst_, hf), in0=cs_, in1=H(a_, hf), op=op), [K(a_, hf), "cb"], [K(dst_, hf)])

                tk.op("pool", lambda e: e.memset(S32[:], 0.0), [], [sid + "S32a", sid + "S32b"])
                tk.op("pool", lambda e: e.memset(Sb[:], 0.0), [], [sid + "Sba", sid + "Sbb"])
                load_blk(0)
                yield
                for n in range(NB):
                    gb = order[n]
                    i = n % 2
                    if n + 1 < NB:
                        load_blk(n + 1)
                    qT, kT, Ktm, Vtm = qf[i], kf[i], ktm[i], vtm[i]
                    qk_, kk_, ktk_, vtk_, gak_ = f"{sid}qf{i}", f"{sid}kf{i}", f"{sid}kt{i}", f"{sid}vt{i}", f"{sid}ga{i}"
                    g_ = ga[i][:, dr * 8:dr * 8 + 8]
                    lnb = ga[i][:, 16 + dr * 8:16 + dr * 8 + 8]
                    bi = gb % 4
                    tt = gb // 4
                    oslot = tt % 2
                    bc = slice(bi * 128, bi * 128 + 128)
                    tk.op("dve", lambda e: e.tensor_tensor(out=rhsD[:], in0=tri.unsqueeze(1).broadcast_to([128, 8, 128]),
                                                           in1=g_.unsqueeze(2).broadcast_to([128, 8, 128]), op=ALU.mult), [gak_, "cf"], [sid + "rhsD"])
                    tk.op("pe", lambda e: e.matmul(PS[0:8, 0:128], lhsT=g_, rhs=tri, start=True, stop=True), [gak_, "cf"], [PSK])
                    yield
                    tk.op("act", lambda e: e.activation(out=ngc[:], in_=PS[0:8, 0:128], func=AF.Copy, scale=-1.0), [PSK], [sid + "ngc"])
                    smm = [(0, tri, g_, True, True), (8, tri, g_, True, False), (8, identf, lnb, False, True), (16, trc, g_, True, True),
                           (24, onesf, g_, True, True), (32, identf, lnb, True, True)]
                    for n_, (c0, L_, R_, s0, s1) in enumerate(smm):
                        tk.op("pe", lambda e: e.matmul(PS[:, 128 + c0:128 + c0 + 8], lhsT=L_, rhs=R_, start=s0, stop=s1), [gak_, "cf"], [PSK], signal=(n_ == 5))
                    yield
                    tk.op("act", lambda e: e.activation(out=ex[:].rearrange("p a b -> p (a b)"), in_=PS[:, 128:168], func=AF.Exp), [PSK], [sid + "ex"])
                    bb = lambda row, hf: ex[:, row, HS[hf]].unsqueeze(2).broadcast_to([128, 4, 128])
                    for hf in range(2):
                        tk.op("dve", lambda e: e.tensor_tensor(out=H("Vb", hf), in0=Vtm[:, HS[hf], :], in1=bb(4, hf), op=ALU.mult), [vtk_, sid + "ex"], [K("Vb", hf)])
                    rD = rhsD[:].rearrange("p h i -> p (h i)")
                    for hf in range(2):
                        tk.op("pe", lambda e: e.matmul(P0[:, hf * 512:(hf + 1) * 512], lhsT=onesf, rhs=rD[:, hf * 512:(hf + 1) * 512], start=True, stop=True),
                              [sid + "rhsD", "cf"], [PHA[hf]])
                    yield
                    for hf in range(2):
                        tk.op("act", lambda e: e.activation(out=H("T1", hf), in_=PH[:, HS[hf], :], func=AF.Exp), [PHA[hf]], [K("T1", hf)])
                        tk.op("pe", lambda e: e.matmul(P0[:, hf * 512:(hf + 1) * 512], lhsT=onesf, rhs=rD[:, hf * 512:(hf + 1) * 512], start=True, stop=False),
                              [sid + "rhsD", "cf"], [PHA[hf]], signal=False)
                        tk.op("pe", lambda e: e.matmul(P0[:, hf * 512:(hf + 1) * 512], lhsT=ngc[:], rhs=self.csel[:, hf * 512:(hf + 1) * 512], start=False, stop=False),
                              [sid + "ngc", "csel"], [PHA[hf]], signal=False)
                        tk.op("pe", lambda e: e.matmul(P0[:, hf * 512:(hf + 1) * 512], lhsT=identb, rhs=mb[:, hf * 512:(hf + 1) * 512], start=False, stop=True),
                              ["cb"], [PHA[hf]])
                    for hf in range(2):
                        tk.op("pool", lambda e: e.tensor_tensor(out=H("qg", hf), in0=qT[:, HS[hf], :], in1=H("T1", hf), op=ALU.mult), [qk_, K("T1", hf)], [K("qg", hf)])
                    yield
                    for hf in range(2):
                        tk.op("act", lambda e: e.activation(out=H("Ds", hf), in_=PH[:, HS[hf], :], func=AF.Exp), [PHA[hf]], [K("Ds", hf)])
                    pool2("Dq", "Ds", idr, ALU.add)
                    hmm(None, None, lambda h: kT[:, h, :], lambda h: kT[:, h, :], [kk_])
                    yield
                    for hf in range(2):
                        tk.op("dve", lambda e: e.tensor_tensor(out=H("M0", hf), in0=PH[:, HS[hf], :], in1=H("Ds", hf), op=ALU.mult), [PHA[hf], K("Ds", hf)], [K("M0", hf)])
                    hmm(None, None, lambda h: kT[:, h, :], lambda h: qT[:, h, :], [kk_, qk_])
                    yield
                    for hf in range(2):
                        tk.op("dve", lambda e: e.scalar_tensor_tensor(H("PTn", hf), PH[:, HS[hf], :], -1.0, H("Dq", hf), ALU.mult, ALU.mult), [PHA[hf], K("Dq", hf)], [K("PTn", hf)])
                    htr(lambda h: H1("M0", h), lambda hf: [K("M0", hf)])
                    yield
                    cpT("A", mul=lambda hf: bb(4, hf))
                    htr(lambda h: H1("A", h), lambda hf: [K("A", hf)])
                    yield
                    cpT("M")
                    pool2("A32x", "A", bd32, ALU.mult)
                    pool2("M32", "M", bd32, ALU.mult)
                    pool2("Ya", "M32", idr, ALU.subtract, a_first=False)
                    pool2("Ao64", "A", off64, ALU.mult)
                    pool2("Ao128", "A", off128, ALU.mult)
                    yield
                    Ac, Mc, Yc = "A32x", "M32", "Ya"
                    for (Mn, An, Yn) in [("Ma", "Aa", "Yb"), ("Mb", "Ab", "Ya"), ("Ma", "Aa", "Yb")]:
                        hmm(Ac, Mc); yield
                        cp(Mn); hmm(Mc, Ac); yield
                        cp(An); hmm(An, Yc); yield
                        yadd(Yc, Yn)
                        Ac, Mc, Yc = An, Mn, Yn
                    hmm(Mc, Ac); yield
                    cp("Ab"); hmm("Ab", Yc); yield
                    yadd(Yc, "Ya")
                    htr(lambda h: H1("Ya", h), lambda hf: [K("Ya", hf)]); yield
                    cpT("X")
                    hmm("Ao64", "Ya"); yield
                    cp("Z"); hmm("X", "Z"); yield
                    yadd("Ya", "Yb", -1.0)
                    htr(lambda h: H1("Yb", h), lambda hf: [K("Yb", hf)]); yield
                    cpT("X")
                    hmm("Ao128", "Yb"); yield
                    cp("Z"); hmm("X", "Z"); yield
                    yadd("Yb", "Ya", -1.0)
                    SK = [sid + "S32a", sid + "S32b"]
                    SBK = [sid + "Sba", sid + "Sbb"]
                    reset = (dr == 0 and gb == NB // 2 - 1) or (dr == 1 and gb == NB // 2)
                    for hf in range(2):
                        for h in range(4 * hf, 4 * hf + 4):
                            tk.op("pe", lambda e: e.matmul(PH[:, h, :], lhsT=kT[:, h, :], rhs=Sb[:, h, :], start=True, stop=True), [kk_, SBK[hf]], [PHA[hf]], signal=(h % 4 == 3))
                    yield
                    for hf in range(2):
                        tk.op("dve", lambda e: e.tensor_tensor(out=tq[hf][:], in0=PH[:, HS[hf], :], in1=bb(1, hf), op=ALU.mult), [PHA[hf], sid + "ex"], [f"{sid}tq{hf}"])
                        tk.op("pool", lambda e: e.tensor_tensor(out=H("Rn", hf), in0=tq[hf][:], in1=H("Vb", hf), op=ALU.subtract), [f"{sid}tq{hf}", K("Vb", hf)], [K("Rn", hf)])
                        for h in range(4 * hf, 4 * hf + 4):
                            tk.op("pe", lambda e: e.matmul(PH[:, h, :], lhsT=H1("Ya", h), rhs=H1("Rn", h), start=True, stop=True), [K("Ya", hf), K("Rn", hf)], [PHA[hf]], signal=(h % 4 == 3))
                    yield
                    for hf in range(2):
                        tk.op("act", lambda e: e.activation(out=H("nvb", hf), in_=PH[:, HS[hf], :], func=AF.Copy), [PHA[hf]], [K("nvb", hf)])
                        tk.op("dve", lambda e: e.tensor_tensor(out=H("nvd", hf), in0=PH[:, HS[hf], :], in1=bb(2, hf), op=ALU.mult), [PHA[hf], sid + "ex"], [K("nvd", hf)])
                        for h in range(4 * hf, 4 * hf + 4):
                            tk.op("pe", lambda e: e.matmul(PH[:, h, :], lhsT=H1("nvb", h), rhs=H1("PTn", h), start=True, stop=False), [K("nvb", hf), K("PTn", hf)], [PHA[hf]], signal=False)
                            tk.op("pe", lambda e: e.matmul(PH[:, h, :], lhsT=Sb[:, h, :], rhs=H1("qg", h), start=False, stop=True), [SBK[hf], K("qg", hf)], [PHA[hf]], signal=(h % 4 == 3))
                    yield
                    for hf in range(2):
                        tk.op("act", lambda e: e.activation(out=ost[oslot][:, HS[hf], bc], in_=PH[:, HS[hf], :], func=AF.Copy), [PHA[hf]], [f"{sid}ost{oslot}"])
                        for h in range(4 * hf, 4 * hf + 4):
                            tk.op("pe", lambda e: e.matmul(PH[:, h, :], lhsT=Ktm[:, h, :], rhs=H1("nvd", h), start=True, stop=True), [ktk_, K("nvd", hf)], [PHA[hf]], signal=(h % 4 == 3))
                    yield
                    for hf in range(2):
                        tk.op("pool", lambda e: e.tensor_tensor(out=tq2[hf][:], in0=S32[:, HS[hf], :], in1=bb(3, hf), op=ALU.mult), [SK[hf], sid + "ex"], [f"{sid}tq2{hf}"])
                        tk.op("dve", lambda e: e.tensor_tensor(out=S32[:, HS[hf], :], in0=tq2[hf][:], in1=PH[:, HS[hf], :], op=ALU.subtract), [f"{sid}tq2{hf}", PHA[hf]], [SK[hf]])
                        if reset:
                            tk.op("dve", lambda e: e.tensor_scalar(S32[:, HS[hf], :], S32[:, HS[hf], :], self.flg[:, 0:1], None, ALU.mult), [SK[hf], "flg"], [SK[hf]])
                        tk.op("pool", lambda e: e.tensor_copy(out=Sb[:, HS[hf], :], in_=S32[:, HS[hf], :]), [SK[hf]], [SBK[hf]])
                    yield
                    last_in_tile = (bi == 3) if dr == 0 else (bi == 0)
                    if last_in_tile:
                        t0 = tt * 512
                        tk.dma(self.OD[dr][:, t0:t0 + 512].rearrange("(h p) t -> p h t", p=128), ost[oslot][:], f"{sid}o{oslot}", reads=[f"{sid}ost{oslot}"])

            gens = [stream(0), stream(1)]
            active = list(gens)
            next(gens[0]); next(gens[1])
            for _ in range(9):
                next(gens[0])
            while active:
                for g in list(active):
                    try:
                        next(g)
                    except StopIteration:
                        active.remove(g)
            tk.barrier()

    def pass3(self, l):
        self.uid += 1
        nc, tk, T = self.nc, self.tk, self.T
        NSB = T // 1024
        NT = self.NT
        with ExitStack() as st:
            sb = lambda n, s, dt: st.enter_context(nc.sbuf_tensor(self.nm(n), list(s), dt))
            ps = lambda n, s, dt: st.enter_context(nc.psum_tensor(self.nm(n), list(s), dt))
            qr = [sb(f"p3q{g}", [128, T], BF16) for g in range(3)]
            kr = [sb(f"p3k{g}", [128, T], BF16) for g in range(3)]
            vt = [sb(f"p3v{g}", [128, T // 128, 130], BF16) for g in range(3)]
            cs = sb("p3cos", [128, 512], F32)
            sn = sb("p3sin", [128, 512], F32)
            sq = [sb(f"p3sq{i}", [128, 512], BF16) for i in range(2)]
            qn = [sb(f"p3qn{i}", [128, 512], BF16) for i in range(2)]
            t1 = [sb(f"p3t1{i}", [128, 512], F32) for i in range(2)]
            t2 = [sb(f"p3t2{i}", [128, 512], F32) for i in range(2)]
            lnr = sb("p3lnr", [12, 512], F32)
            rs12 = lnr
            wq = sb("p3wq", [128, 1], F32)
            wk_ = sb("p3wk", [128, 1], F32)
            wrow = sb("p3wrow", [1, 128], F32)
            mx = sb("p3mx", [1, 4], F32)
            negB = sb("p3negB", [128, 1], F32)
            segrow = sb("p3seg", [1, 64], F32)
            pT = [sb(f"p3pT{i}", [128, 512], BF16) for i in range(2)]
            zb = [sb(f"p3zb{i}", [64, 1024], BF16) for i in range(2)]
            rrow = sb("p3rrow", [65, 1024], F32)
            bcs = sb("p3bcs", [64, 1024], F32)
            og = sb("p3og", [64, 1024], BF16)
            PN = [ps(f"p3PN{i}", [65, 1024], F32) for i in range(2)]
            PSC = [ps(f"p3PS{i}", [128, 512], F32) for i in range(2)]
            PP = [ps(f"p3PP{i}", [128, 512], F32) for i in range(2)]
            identb = self.CB(CB_ID)
            rotT = self.CB(CB_ROT)
            oh12 = self.cb[:, CB_OH12 * 128:CB_OH12 * 128 + 72].rearrange("p (c m) -> p c m", m=12)
            amc = self.cb[:, CB_AMC * 128:CB_AMC * 128 + 704]
            amb = self.cb[:, CB_AMB * 128:CB_AMB * 128 + 512]
            amd = self.cb[:, CB_AMD * 128:CB_AMD * 128 + 512]
            tk.dma(wq[:], self.qnw_c[l], "p3s", writes=["wq"])
            tk.dma(wk_[:], self.knw_c[l], "p3s", writes=["wk"])
            tk.dma(wrow[:, 0:64], self.qnw_r[l], "p3s", writes=["wrow"])
            tk.dma(wrow[:, 64:128], self.knw_r[l], "p3s", writes=["wrow"])
            tk.op("dve", lambda e: e.tensor_scalar(wq[:], wq[:], 0.125, None, ALU.mult), ["wq"], ["wq"])
            tk.op("dve", lambda e: e.tensor_reduce(out=mx[:, 0:1], in_=wrow[:, 0:64], axis=AX.X, op=ALU.max, apply_absolute_value=True), ["wrow"], ["mx"])
            tk.op("dve", lambda e: e.tensor_reduce(out=mx[:, 1:2], in_=wrow[:, 64:128], axis=AX.X, op=ALU.max, apply_absolute_value=True), ["wrow"], ["mx"])
            tk.op("dve", lambda e: e.scalar_tensor_tensor(mx[:, 2:3], mx[:, 0:1], -8.0, mx[:, 1:2], ALU.mult, ALU.mult), ["mx"], ["mx"])
            tk.op("pe", lambda e: e.matmul(PP[0][:, 0:1], lhsT=self.cf[0:1, CF_ONES * 128:(CF_ONES + 1) * 128], rhs=mx[0:1, 2:3], start=True, stop=True), ["mx", "cf"], ["PP0"])
            tk.op("dve", lambda e: e.tensor_copy(out=negB[:, 0:1], in_=PP[0][:, 0:1]), ["PP0"], ["negB"])
            tk.op("dve", lambda e: e.tensor_copy(out=segrow[:], in_=self.flg[0:1, 1:2].broadcast_to([1, 64])), ["flg"], ["segrow"])
            for g in range(3):
                tk.op("pool", lambda e: e.memset(vt[g][:, :, 64:65], 1.0), [], [f"vt{g}"])
                tk.op("pool", lambda e: e.memset(vt[g][:, :, 129:130], 1.0), [], [f"vt{g}"])
            DIL = (1, 4, 16)
            sci = [0]
            pni = [0]
            QK = lambda g, t: f"qr{g}_{t}"
            KK = lambda g, t: f"kr{g}_{t}"

            def tkeys(fn, g, c0, n, d):
                return [fn(g, t) for t in range(c0 // 512, (c0 + (n - 1) * d) // 512 + 1)]

            for hp in range(4):
                for g in range(3):
                    for c0 in range(0, T, 2048):
                        c1 = min(T, c0 + 2048)
                        tl = list(range(c0 // 512, c1 // 512))
                        tk.dma(qr[g][:, c0:c1], self.QKB[g * 512 + hp * 128:g * 512 + hp * 128 + 128, c0:c1], "p3lq", writes=[QK(g, t) for t in tl])
                        tk.dma(kr[g][:, c0:c1], self.QKB[1536 + g * 512 + hp * 128:1536 + g * 512 + hp * 128 + 128, c0:c1], "p3lq", writes=[KK(g, t) for t in tl])
                    d = DIL[g]
                    nt = T // 128 // d
                    for ee in range(2):
                        c0v = g * 512 + hp * 128 + 64 * ee
                        src = self.VB[:, c0v:c0v + 64].rearrange("(j p dd) c -> dd p j c", dd=d, p=128)
                        for z in range(d):
                            for j0 in range(0, nt, 8):
                                j1 = min(nt, j0 + 8)
                                tk.dma(vt[g][:, z * nt + j0:z * nt + j1, 65 * ee:65 * ee + 64], src[z][:, j0:j1, :], "p3lv", writes=[f"vt{g}"])
                for tt in range(NT):
                    t0 = tt * 512
                    tc_ = slice(t0, t0 + 512)
                    tk.dma(cs[:], self.c_cos[:, tc_], "p3cs", writes=["cs"])
                    tk.dma(sn[:], self.c_sin[:, tc_], "p3cs", writes=["sn"])
                    streams = [(qr[g], QK(g, tt), wq) for g in range(3)] + [(kr[g], KK(g, tt), wk_) for g in range(3)]
                    for c, (buf, bk, w_) in enumerate(streams):
                        tk.op("act", lambda e: e.activation(out=sq[c % 2][:], in_=buf[:, tc_], func=AF.Square), [bk], [f"sq{c % 2}"])
                        tk.op("pe", lambda e: e.matmul(PN[0][0:12, 0:512], lhsT=oh12[:, c, :], rhs=sq[c % 2][:], start=(c == 0), stop=(c == 5)),
                              [f"sq{c % 2}", "cb"], ["PN0"])
                    tk.op("act", lambda e: e.activation(out=lnr[:], in_=PN[0][0:12, 0:512], func=AF.Ln, scale=1.0 / 64.0, bias=self.epsc[0:12, 0:1]), ["PN0"], ["lnr", "rs12"])
                    tk.op("act", lambda e: e.activation(out=rs12[:], in_=lnr[:], func=AF.Exp, scale=-0.5), ["lnr"], ["lnr", "rs12"])
                    for c, (buf, bk, w_) in enumerate(streams):
                        j = c % 2
                        pbc, pbck = (PP[0], "PP0") if j == 0 else (PSC[0], "PSC0")
                        prt, prtk = (PP[1], "PP1") if j == 0 else (PSC[1], "PSC1")
                        tk.op("pe", lambda e: e.matmul(pbc[:], lhsT=self.csel12[:, c * 128:(c + 1) * 128], rhs=rs12[:], start=True, stop=True), ["rs12", "csel12"], [pbck])
                        tk.op("dve", lambda e: e.scalar_tensor_tensor(qn[j][:], buf[:, tc_], w_[:, 0:1], pbc[:], ALU.mult, ALU.mult), [pbck, bk, "wq", "wk"], [f"qn{j}"])
                        tk.op("pe", lambda e: e.matmul(prt[:], lhsT=rotT, rhs=qn[j][:], start=True, stop=True), [f"qn{j}", "cb"], [prtk])
                        tk.op("pool", lambda e: e.tensor_tensor(out=t1[j][:], in0=qn[j][:], in1=cs[:], op=ALU.mult), [f"qn{j}", "cs"], [f"t1{j}"])
                        tk.op("dve", lambda e: e.tensor_tensor(out=t2[j][:], in0=prt[:], in1=sn[:], op=ALU.mult), [prtk, "sn"], [f"t2{j}"])
                        tk.op("pool", lambda e: e.tensor_tensor(out=buf[:, tc_], in0=t1[j][:], in1=t2[j][:], op=ALU.add), [f"t1{j}", f"t2{j}"], [bk])
                for sbk in range(NSB):
                    s0 = sbk * 1024
                    for e_ in range(2):
                        hd = 2 * hp + e_
                        pr = slice(64 * e_, 64 * e_ + 64)
                        vc = slice(65 * e_, 65 * e_ + 65)
                        zslot = (sbk * 2 + e_) % 2
                        pn_i = pni[0] % 2
                        pni[0] += 1
                        PNc = PN[pn_i]
                        PNK = f"PN{pn_i}"
                        tk.dma(zb[zslot][:], self.ZB[hd * 64:hd * 64 + 64, s0:s0 + 1024], f"p3z{zslot}", writes=[f"zb{zslot}"])
                        batch = []
                        pending = []
                        tk.op("dve", lambda e: e.memset(PNc[:], 0.0), [], [PNK])

                        def pattern(mvs):
                            n = len(mvs)
                            if all(mvs[i] == (mvs[0] + i) % 4 for i in range(n)):
                                return amc[:, mvs[0] * 64:(mvs[0] + n) * 64]
                            if all(mvs[i] == (2, 3)[i % 2] for i in range(n)):
                                return amb[:, 0:n * 64]
                            if all(mvs[i] == (0, 1)[i % 2] for i in range(n)):
                                return amd[:, 0:n * 64]
                            return None

                        def flush():
                            if not batch:
                                return
                            slot = sci[0] % 2
                            sci[0] += 1
                            nb_ = len(batch)
                            n = nb_ * 64
                            mrhs = pattern([b_[5] for b_ in batch])
                            assert mrhs is not None
                            tk.op("pe", lambda e: e.matmul(PSC[slot][:, 0:n], lhsT=identb, rhs=mrhs, start=True, stop=False), ["cb"], [f"PSC{slot}"], signal=False)
                            for i_, (seg, g, kc0, qc0, d, mv, vti, ocs) in enumerate(batch):
                                cols = slice(i_ * 64, i_ * 64 + 64)
                                kcols = slice(kc0, kc0 + 127 * d + 1, d)
                                qcols = slice(qc0, qc0 + 63 * d + 1, d)
                                last = (i_ == nb_ - 1)
                                if seg:
                                    tk.op("pe", lambda e: e.matmul(PSC[slot][:, cols], lhsT=self.cf[0:1, CF_ONES * 128:(CF_ONES + 1) * 128], rhs=segrow[0:1, :], start=False, stop=False),
                                          ["cf", "segrow"], [f"PSC{slot}"], signal=False)
                                tk.op("pe", lambda e: e.matmul(PSC[slot][:, cols], lhsT=kr[g][pr, kcols], rhs=qr[g][pr, qcols], start=False, stop=last),
                                      tkeys(KK, g, kc0, 128, d) + tkeys(QK, g, qc0, 64, d), [f"PSC{slot}"], signal=last)
                            tk.op("act", lambda e: e.activation(out=pT[slot][:, 0:n], in_=PSC[slot][:, 0:n], func=AF.Exp, bias=negB[:, 0:1]),
                                  [f"PSC{slot}", "negB"], [f"pT{slot}"])
                            prev = list(pending)
                            pending.clear()
                            pending.append((slot, list(batch)))
                            batch.clear()
                            for (pslot_, pb) in prev:
                                emit_pv(pslot_, pb)

                        def emit_pv(slot, pb):
                            nb_ = len(pb)
                            for i_, (seg, g, kc0, qc0, d, mv, vti, ocs) in enumerate(pb):
                                for oi, (oc0, n_o, pc0) in enumerate(ocs):
                                    ocols = slice(oc0, oc0 + (n_o - 1) * d + 1, d)
                                    lastpv = (i_ == nb_ - 1 and oi == len(ocs) - 1)
                                    tk.op("pe", lambda e: e.matmul(PNc[:, ocols], lhsT=vt[g][:, vti, vc], rhs=pT[slot][:, i_ * 64 + pc0:i_ * 64 + pc0 + n_o],
                                                                   start=False, stop=False, skip_group_check=True),
                                          [f"vt{g}", f"pT{slot}"], [PNK], signal=lastpv)

                        for g in range(3):
                            d = DIL[g]
                            L = T // d
                            nt = L // 128
                            lb = (T // 2) // d
                            nun = 1024 // d // 64
                            u0 = (s0 // d) // 64
                            for z in range(d):
                                for u in range(u0, u0 + nun):
                                    if u % 2 == 1:
                                        cand = [((u - 1) // 2, 0), ((u + 1) // 2, 1)]
                                    else:
                                        cand = [(u // 2 - 1, 2), (u // 2, 3)]
                                    for (j, mv) in cand:
                                        if j < 0 or j >= nt:
                                            continue
                                        seg = 1 if ((128 * j >= lb) != (64 * u >= lb)) else 0
                                        qc0 = z + d * 64 * u
                                        kc0 = z + d * 128 * j
                                        oc0 = qc0 - s0
                                        if d == 16:
                                            ocs = [(oc0, 32, 0), (oc0 + 512, 32, 32)]
                                        else:
                                            ocs = [(oc0, 64, 0)]
                                        if batch and (len(batch) == 8 or pattern([b_[5] for b_ in batch] + [mv]) is None):
                                            flush()
                                        batch.append((seg, g, kc0, qc0, d, mv, z * nt + j, ocs))
                            flush()
                        for (pslot_, pb) in pending:
                            emit_pv(pslot_, pb)
                        pending.clear()
                        tk.op("act", lambda e: e.activation(out=rrow[64:65, :], in_=PNc[64:65, :], func=AF.Ln), [PNK], ["rrow"])
                        tk.op("act", lambda e: e.activation(out=rrow[64:65, :], in_=rrow[64:65, :], func=AF.Exp, scale=-1.0), ["rrow"], ["rrow"])
                        for hf in range(2):
                            tk.op("pe", lambda e: e.matmul(PP[hf][0:64, :], lhsT=self.cf[64:65, CF_ONES * 128:CF_ONES * 128 + 64], rhs=rrow[64:65, hf * 512:(hf + 1) * 512], start=True, stop=True),
                                  ["rrow", "cf"], [f"PP{hf}"])
                            tk.op("act", lambda e: e.activation(out=bcs[:, hf * 512:(hf + 1) * 512], in_=PP[hf][0:64, :], func=AF.Copy), [f"PP{hf}"], ["bcs"])
                        tk.op("dve", lambda e: e.tensor_tensor(out=bcs[:], in0=PNc[0:64, :], in1=bcs[:], op=ALU.mult), [PNK, "bcs"], ["bcs"])
                        tk.op("pool", lambda e: e.tensor_tensor(out=og[:], in0=bcs[:], in1=zb[zslot][:], op=ALU.mult), ["bcs", f"zb{zslot}"], ["og"])
                        tk.dma(self.OGB[hd * 64:hd * 64 + 64, s0:s0 + 1024], og[:], "p3o", reads=["og"])
            tk.barrier()

    def pass4(self, l, xin, xout):
        self.uid += 1
        nc, tk, T = self.nc, self.tk, self.T
        with ExitStack() as st:
            sb = lambda n, s, dt: st.enter_context(nc.sbuf_tensor(self.nm(n), list(s), dt))
            ps = lambda n, s, dt: st.enter_context(nc.psum_tensor(self.nm(n), list(s), dt))
            wa = sb("p4wa", [128, 8, 1024], BF16)
            wbt = sb("p4wb", [128, 4, 1024], BF16)
            wo = sb("p4wo", [128, 8, 1024], BF16)
            of = sb("p4of", [128, 8, 512], BF16)
            ob = sb("p4ob", [128, 8, 512], BF16)
            za = sb("p4za", [128, 8, 512], BF16)
            ogb = [sb(f"p4gb{i}", [128, 4, 512], BF16) for i in range(2)]
            mg = sb("p4mg", [128, 16, 512], BF16)
            xt = [sb(f"p4x{i}", [128, 1024], F32) for i in range(4)]
            xo = [sb(f"p4xo{i}", [128, 1024], F32) for i in range(2)]
            o32 = sb("p4o", [128, 8, 512], F32)
            sq = [sb(f"p4sq{i}", [128, 512], BF16) for i in range(2)]
            lnr = sb("p4lnr", [8, 512], F32)
            rs8 = sb("p4rs8", [8, 512], F32)
            tmp = [sb(f"p4tm{i}", [128, 512], F32) for i in range(2)]
            og = sb("p4og", [128, 8, 512], BF16)
            t1 = [sb(f"p4t1{i}", [128, 512], F32) for i in range(2)]
            t2 = [sb(f"p4t2{i}", [128, 512], F32) for i in range(2)]
            mT = sb("p4mT", [128, 8, 512], BF16)
            pms = [ps(f"p4ms{i}", [128, 512], F32) for i in range(2)]
            pya = [ps(f"p4ya{i}", [128, 512], F32) for i in range(2)]
            pyb = [ps(f"p4yb{i}", [128, 512], F32) for i in range(2)]
            pd = [ps(f"p4pd{i}", [128, 512], F32) for i in range(2)]
            tk.dma(wa[:], self.WA[l], "p4w", writes=["wa"])
            tk.dma(wbt[:], self.WB[l], "p4w", writes=["wb"])
            tk.dma(wo[:], self.WO[l], "p4w", writes=["wo"])
            oh8 = self.cb[:, CB_OH8 * 128:CB_OH8 * 128 + 64].rearrange("p (h m) -> p h m", m=8)

            def fm(dr, n, t0):
                return dr[0:n * 128, t0:t0 + 512].rearrange("(h p) t -> p h t", p=128)

            def loads_a(tt):
                t0 = tt * 512
                tk.dma(of[:], fm(self.OD[0], 8, t0), "p4a", writes=["of"])
                tk.dma(ob[:], fm(self.OD[1], 8, t0), "p4a", writes=["ob"])
                tk.dma(za[:], fm(self.ZA, 8, t0), "p4b", writes=["za"])

            def loads_b(tt):
                t0 = tt * 512
                i = tt % 2
                tk.dma(ogb[i][:], fm(self.OGB, 4, t0), f"p4g{i}", writes=[f"ogb{i}"])
                tk.dma(mg[:], fm(self.MG, 16, t0), "p4c", writes=["mg"])

            loads_a(0)
            loads_b(0)
            xi = [0]
            for tt in range(self.NT):
                t0 = tt * 512
                i = tt % 2
                for h in range(8):
                    j = h % 2
                    tk.op("pool", lambda e: e.tensor_tensor(out=o32[:, h, :], in0=of[:, h, :], in1=ob[:, h, :], op=ALU.add), ["of", "ob"], [f"o32{h}"])
                    tk.op("act", lambda e: e.activation(out=sq[j][:], in_=o32[:, h, :], func=AF.Square), [f"o32{h}"], [f"sq{j}"])
                    tk.op("pe", lambda e: e.matmul(pms[0][0:8, :], lhsT=oh8[:, h, :], rhs=sq[j][:], start=(h == 0), stop=(h == 7)), [f"sq{j}", "cb"], ["pms0"])
                tk.op("act", lambda e: e.activation(out=lnr[:], in_=pms[0][0:8, :], func=AF.Ln, scale=1.0 / 128.0, bias=self.epsc[0:8, 2:3]), ["pms0"], ["lnr"])
                tk.op("act", lambda e: e.activation(out=rs8[:], in_=lnr[:], func=AF.Exp, scale=-0.5), ["lnr"], ["rs8"])
                for h in range(8):
                    j = h % 2
                    tk.op("pe", lambda e: e.matmul(pms[1][:], lhsT=self.csel[:, h * 128:(h + 1) * 128], rhs=rs8[:], start=True, stop=True), ["rs8", "csel"], ["pms1"])
                    tk.op("dve", lambda e: e.tensor_tensor(out=tmp[j][:], in0=pms[1][:], in1=o32[:, h, :], op=ALU.mult), ["pms1", f"o32{h}"], [f"tmp{j}"])
                    tk.op("pool", lambda e: e.tensor_tensor(out=og[:, h, :], in0=tmp[j][:], in1=za[:, h, :], op=ALU.mult), [f"tmp{j}", "za"], [f"og{h}"])
                if tt + 1 < self.NT:
                    loads_a(tt + 1)
                xsl = []
                for s in range(4):
                    xs = xi[0] % 4
                    xi[0] += 1
                    xsl.append(xs)
                    tk.dma(xt[xs][:], xin[t0 + s * 128:t0 + (s + 1) * 128, :], f"p4x{xs}", writes=[f"x{xs}"])
                for oc in range(8):
                    j = oc % 2
                    for kc in range(8):
                        tk.op("pe", lambda e: e.matmul(pya[j][:], lhsT=wa[:, kc, oc * 128:(oc + 1) * 128], rhs=og[:, kc, :], start=(kc == 0), stop=(kc == 7)),
                              ["wa", f"og{kc}"], [f"pya{j}"], signal=(kc == 7))
                    for kc in range(4):
                        tk.op("pe", lambda e: e.matmul(pyb[j][:], lhsT=wbt[:, kc, oc * 128:(oc + 1) * 128], rhs=ogb[i][:, kc, :], start=(kc == 0), stop=(kc == 3)),
                              ["wb", f"ogb{i}"], [f"pyb{j}"], signal=(kc == 3))
                    tk.op("dve", lambda e: e.scalar_tensor_tensor(t1[j][:], mg[:, oc, :], 1.0, pya[j][:], ALU.add, ALU.mult), [f"pya{j}", "mg"], [f"t1{j}"])
                    tk.op("dve", lambda e: e.scalar_tensor_tensor(t2[j][:], mg[:, 8 + oc, :], 1.0, pyb[j][:], ALU.add, ALU.mult), [f"pyb{j}", "mg"], [f"t2{j}"])
                    tk.op("pool", lambda e: e.tensor_tensor(out=mT[:, oc, :], in0=t1[j][:], in1=t2[j][:], op=ALU.add), [f"t1{j}", f"t2{j}"], [f"mT{oc}"])
                if tt + 1 < self.NT:
                    loads_b(tt + 1)
                for s in range(4):
                    xs = xsl[s]
                    xos = (tt * 4 + s) % 2
                    for half in range(2):
                        j = half
                        for kc in range(8):
                            tk.op("pe", lambda e: e.matmul(pd[j][:], lhsT=mT[:, kc, s * 128:(s + 1) * 128], rhs=wo[:, kc, half * 512:(half + 1) * 512],
                                                           start=(kc == 0), stop=(kc == 7)), ["wo", f"mT{kc}"], [f"pd{j}"], signal=(kc == 7))
                        tk.op("dve", lambda e: e.tensor_tensor(out=xo[xos][:, half * 512:(half + 1) * 512], in0=pd[j][:],
                                                               in1=xt[xs][:, half * 512:(half + 1) * 512], op=ALU.add), [f"pd{j}", f"x{xs}"], [f"xo{xos}"])
                    tk.dma(xout[t0 + s * 128:t0 + (s + 1) * 128, :], xo[xos][:], f"p4o{xos}", reads=[f"xo{xos}"])
            tk.barrier()


CF_ID, CF_ONES, CF_TRI0, CF_TRI1, CF_TRC0, CF_TRC1 = range(6)
NCF = 6
(CB_ID, CB_ONES, CB_ROT, CB_OH8, CB_OH12) = range(5)
CB_MB0 = 5
CB_MB1 = 13
CB_IDR = 21
CB_BD32 = 29
CB_OFF64 = 37
CB_OFF128 = 45
CB_AMC = 53
CB_AMB = 59
CB_AMD = 63
NCB = 67


def make_consts(T):
    j = np.arange(128)[:, None]
    i = np.arange(128)[None, :]
    cf = np.zeros((128, NCF, 128), np.float32)
    cf[:, CF_ID] = np.eye(128)
    cf[:, CF_ONES] = 1.0
    cf[:, CF_TRI0] = (j <= i)
    cf[:, CF_TRI1] = (j >= i)
    cf[:, CF_TRC0] = 1.0 - cf[:, CF_TRI0]
    cf[:, CF_TRC1] = 1.0 - cf[:, CF_TRI1]
    cb = np.zeros((128, NCB, 128), np.float32)
    cb[:, CB_ID] = np.eye(128)
    cb[:, CB_ONES] = 1.0
    R = np.zeros((128, 128), np.float32)
    for b in (0, 64):
        for d in range(32):
            R[b + d, b + d + 32] = -1.0
            R[b + d + 32, b + d] = 1.0
    cb[:, CB_ROT] = R.T
    oh8 = np.zeros((128, 8, 8), np.float32)
    for h in range(8):
        oh8[:, h, h] = 1.0
    cb[:, CB_OH8].reshape(128, 128)[:, :64] = oh8.reshape(128, 64)
    oh12 = np.zeros((128, 6, 12), np.float32)
    for c in range(6):
        oh12[:64, c, 2 * c] = 1.0
        oh12[64:, c, 2 * c + 1] = 1.0
    cb[:, CB_OH12].reshape(128, 128)[:, :72] = oh12.reshape(128, 72)
    for h in range(8):
        cb[:, CB_MB0 + h] = np.where(i > j, 0.0, NEG)
        cb[:, CB_MB1 + h] = np.where(i < j, 0.0, NEG)
        cb[:, CB_IDR + h] = np.eye(128)
        cb[:, CB_BD32 + h] = ((j // 32) == (i // 32))
        cb[:, CB_OFF64 + h] = ((j // 64) == (i // 64)) & ((j // 32) != (i // 32))
        cb[:, CB_OFF128 + h] = ((j // 64) != (i // 64))
    a = np.arange(128)[:, None]
    b = np.arange(64)[None, :]
    lo, hi = (a < 64), (a >= 64)
    am = np.zeros((4, 128, 64), np.float32)
    am[0] = np.where((lo & (b <= a)) | hi, 0.0, NEG)
    am[1] = np.where(lo & (a <= b), 0.0, NEG)
    am[2] = np.where(hi & (b <= (a - 64)), 0.0, NEG)
    am[3] = np.where(lo | (hi & ((a - 64) <= b)), 0.0, NEG)
    cb[:, CB_AMC:CB_AMC + 6].reshape(128, 768)[:, :704] = np.concatenate([am[i % 4] for i in range(11)], axis=1)
    cb[:, CB_AMB:CB_AMB + 4].reshape(128, 512)[:, :] = np.concatenate([am[(2, 3)[i % 2]] for i in range(8)], axis=1)
    cb[:, CB_AMD:CB_AMD + 4].reshape(128, 512)[:, :] = np.concatenate([am[(0, 1)[i % 2]] for i in range(8)], axis=1)
    sel = np.zeros((8, 8, 128), np.float32)
    for h in range(8):
        sel[h, h, :] = 1.0
    sel12 = np.zeros((12, 6, 128), np.float32)
    for c in range(6):
        sel12[2 * c, c, :64] = 1.0
        sel12[2 * c + 1, c, 64:] = 1.0
    half = 32
    inv = (10000.0 ** (-np.arange(half, dtype=np.float32) / half)).astype(np.float32)
    pos = np.arange(T, dtype=np.float32)
    ang = (pos[None, :] * inv[np.arange(128) % 32][:, None]).astype(np.float32)
    epsc = np.zeros((128, 4), np.float32)
    epsc[:, 0] = EPS
    epsc[:, 1] = 128.0 * 4.0 * EPS
    epsc[:, 2] = 4.0 * EPS
    epsc[:, 3] = 1.0
    return dict(c_f32=cf.reshape(128, NCF * 128), c_bf=cb.reshape(128, NCB * 128).astype(ml_dtypes.bfloat16),
                c_sel=sel.reshape(8, 1024), c_sel12=sel12.reshape(12, 768), c_eps=epsc,
                c_cos=np.cos(ang).astype(np.float32), c_sin=np.sin(ang).astype(np.float32))


def layout_params(p, depth):
    out = {}
    out["w_in"] = np.ascontiguousarray(p["w_in"][:depth], dtype=np.float32)
    out["norm_w8"] = np.ascontiguousarray(p["norm_w"][:depth].reshape(depth, 8, 128).transpose(0, 2, 1), dtype=np.float32)
    out["conv_l"] = np.ascontiguousarray(p["conv_a"][:depth].reshape(depth, 5, 24, 128).transpose(0, 3, 2, 1).reshape(depth, 128, 120), dtype=np.float32)
    out["alog_r"] = np.ascontiguousarray(np.broadcast_to(p["a_log"][:depth].reshape(depth, 1, 16), (depth, 128, 16)), dtype=np.float32)
    out["dtb_r"] = np.ascontiguousarray(np.broadcast_to(p["dt_bias"][:depth].reshape(depth, 1, 16), (depth, 128, 16)), dtype=np.float32)
    out["anw_c"] = np.ascontiguousarray(p["a_norm_w"][:depth].reshape(depth, 128, 1), dtype=np.float32)
    out["qnw_c"] = np.ascontiguousarray(np.concatenate([p["q_norm_w"][:depth]] * 2, axis=1).reshape(depth, 128, 1), dtype=np.float32)
    out["knw_c"] = np.ascontiguousarray(np.concatenate([p["k_norm_w"][:depth]] * 2, axis=1).reshape(depth, 128, 1), dtype=np.float32)
    out["qnw_r"] = np.ascontiguousarray(p["q_norm_w"][:depth].reshape(depth, 1, 64), dtype=np.float32)
    out["knw_r"] = np.ascontiguousarray(p["k_norm_w"][:depth].reshape(depth, 1, 64), dtype=np.float32)
    out["w_a_out"] = np.ascontiguousarray(p["w_a_out"][:depth], dtype=np.float32)
    out["w_b_out"] = np.ascontiguousarray(p["w_b_out"][:depth], dtype=np.float32)
    out["w_out"] = np.ascontiguousarray(p["w_out"][:depth], dtype=np.float32)
    return out


def flags_for(f):
    fl = np.zeros((128, 2), np.float32)
    fl[:, 0] = f
    fl[:, 1] = NEG * (1.0 - f)
    return fl


def kernel(x_prompt, x_sample, norm_w, w_in, conv_a, a_log, dt_bias, a_norm_w, q_norm_w, k_norm_w, w_a_out, w_b_out, w_out):
    T, depth = 8192, 4
    prm = dict(norm_w=np.asarray(norm_w), w_in=np.asarray(w_in), conv_a=np.asarray(conv_a), a_log=np.asarray(a_log), dt_bias=np.asarray(dt_bias),
               a_norm_w=np.asarray(a_norm_w), q_norm_w=np.asarray(q_norm_w), k_norm_w=np.asarray(k_norm_w), w_a_out=np.asarray(w_a_out),
               w_b_out=np.asarray(w_b_out), w_out=np.asarray(w_out))
    shared = layout_params(prm, depth)
    shared.update(make_consts(T))
    xp = np.asarray(x_prompt, dtype=np.float32)
    xs = np.asarray(x_sample, dtype=np.float32)
    in_maps = []
    for c in range(8):
        m = dict(shared)
        if c < 4:
            m["x"] = np.ascontiguousarray(xp[c])
            m["flags"] = flags_for(1.0)
        else:
            m["x"] = np.ascontiguousarray(xs[2 * (c - 4):2 * (c - 4) + 2].reshape(T, D))
            m["flags"] = flags_for(0.0)
        in_maps.append(m)
    nc = Prog(T, depth).build()
    res = run_bass_kernel_spmd(nc, in_maps, core_ids=list(range(8)))
    yp = np.stack([np.asarray(res.results[c]["y"]) for c in range(4)], axis=0).astype(np.float32)
    ys = np.concatenate([np.asarray(res.results[c]["y"]).reshape(2, 4096, D) for c in range(4, 8)], axis=0).astype(np.float32)
    return (yp, ys)
```
